# Optimizing a Trainium2 kernel written in Bass

```python
import jax, jax.numpy as jnp
from jax import lax
import numpy as np

D_MODEL = 1024
BATCH = 4
SEQ = 4096
DEPTH = 4

HEAD_DIM = 64
N_HEADS_TOTAL = D_MODEL // HEAD_DIM
N_MEM_HEADS = 4
N_MIX_HEADS = N_HEADS_TOTAL - N_MEM_HEADS
D_MIX = N_MIX_HEADS * HEAD_DIM
D_MEMQ = N_MEM_HEADS * HEAD_DIM
MEM_TOKENS = 256
D_FF = 2816
CONV_WIDTH = 3
ROPE_THETA = 10000.0
Q_BLOCK = 128
DILATED_BRANCHES = ((128, 1), (512, 4), (2048, 16))
N_MIXERS = 2
N_A_LAYERS = (DEPTH + 1) // 2
N_B_LAYERS = DEPTH // 2
FOX_IN = 3 * D_MIX + N_MIX_HEADS + D_MEMQ
DIL_IN = 3 * D_MIX + D_MEMQ
NORM_EPS = 1e-6
NEG = -1e30

kernel_name = 'hybrid_fox_dilated_memory_convffn'


def rmsnorm(x, g):
    xf = x.astype(jnp.float32)
    y = xf * lax.rsqrt(jnp.mean(xf * xf, axis=-1, keepdims=True) + NORM_EPS)
    return (y * g.astype(jnp.float32)).astype(x.dtype)


def split_heads(t, n_heads):
    b, s, _ = t.shape
    return t.reshape(b, s, n_heads, HEAD_DIM).transpose(0, 2, 1, 3)


def merge_heads(t):
    b, h, s, d = t.shape
    return t.transpose(0, 2, 1, 3).reshape(b, s, h * d)


def rope_tables(seq):
    inv = 1.0 / (ROPE_THETA ** (jnp.arange(0, HEAD_DIM, 2, dtype=jnp.float32) / HEAD_DIM))
    ang = jnp.arange(seq, dtype=jnp.float32)[:, None] * inv[None, :]
    return jnp.cos(ang), jnp.sin(ang)


def apply_rope(x, cos, sin):
    x1, x2 = jnp.split(x.astype(jnp.float32), 2, axis=-1)
    y = jnp.concatenate([x1 * cos - x2 * sin, x2 * cos + x1 * sin], axis=-1)
    return y.astype(x.dtype)


def fox_attention(q, k, v, log_f):
    b, h, s, dh = q.shape
    c = lax.cumsum(log_f, axis=2)
    nb = s // Q_BLOCK
    scale = dh ** -0.5
    qb = q.reshape(b, h, nb, Q_BLOCK, dh).transpose(2, 0, 1, 3, 4)
    cb = c.reshape(b, h, nb, Q_BLOCK).transpose(2, 0, 1, 3)
    starts = jnp.arange(nb, dtype=jnp.int32) * Q_BLOCK
    kpos = jnp.arange(s, dtype=jnp.int32)

    def one_block(args):
        qi, ci, st = args
        sc = jnp.einsum('bhqd,bhkd->bhqk', qi, k, preferred_element_type=jnp.float32) * scale
        sc = sc + ci[..., :, None] - c[..., None, :]
        qpos = st + jnp.arange(Q_BLOCK, dtype=jnp.int32)
        sc = jnp.where(kpos[None, :] <= qpos[:, None], sc, NEG)
        p = jax.nn.softmax(sc, axis=-1)
        return jnp.einsum('bhqk,bhkd->bhqd', p.astype(v.dtype), v)

    out = lax.map(one_block, (qb, cb, starts))
    return out.transpose(1, 2, 0, 3, 4).reshape(b, h, s, dh)


def dilated_branch(q, k, v, window, dilation):
    b, h, s, dh = q.shape
    L = window // dilation
    chunk = dilation * L
    sp = -(-s // chunk) * chunk
    n = sp // dilation
    nb = n // L
    scale = dh ** -0.5

    def strided(t):
        t = jnp.pad(t, ((0, 0), (0, 0), (0, sp - s), (0, 0)))
        t = t.reshape(b, h, n, dilation, dh).transpose(0, 1, 3, 2, 4)
        return t.reshape(b, h, dilation, nb, L, dh)

    def with_prev(t):
        prev = jnp.pad(t, ((0, 0), (0, 0), (0, 0), (1, 0), (0, 0), (0, 0)))[:, :, :, :-1]
        return jnp.concatenate([prev, t], axis=4)

    qs = strided(q)
    kb = with_prev(strided(k))
    vb = with_prev(strided(v))
    sc = jnp.einsum('bhrnqd,bhrnkd->bhrnqk', qs, kb, preferred_element_type=jnp.float32) * scale
    a = jnp.arange(L, dtype=jnp.int32)
    cidx = jnp.arange(2 * L, dtype=jnp.int32)
    blk = jnp.arange(nb, dtype=jnp.int32)
    dist = a[:, None] + L - cidx[None, :]
    valid = (dist >= 0) & (dist <= L)
    valid = valid[None] & ((blk[:, None, None] > 0) | (cidx[None, None, :] >= L))
    sc = jnp.where(valid, sc, NEG)
    m = jnp.max(sc, axis=-1, keepdims=True)
    e = jnp.exp(sc - m)
    den = jnp.sum(e, axis=-1, keepdims=True)
    o = jnp.einsum('bhrnqk,bhrnkd->bhrnqd', (e / den).astype(v.dtype), vb)
    lse = (m + jnp.log(den))[..., 0]
    o = o.reshape(b, h, dilation, n, dh).transpose(0, 1, 3, 2, 4).reshape(b, h, sp, dh)[:, :, :s]
    lse = lse.reshape(b, h, dilation, n).transpose(0, 1, 3, 2).reshape(b, h, sp)[:, :, :s]
    return o, lse


def dilated_attention(q, k, v):
    outs, lses = [], []
    for window, dilation in DILATED_BRANCHES:
        o, lse = dilated_branch(q, k, v, window, dilation)
        outs.append(o)
        lses.append(lse)
    wts = jax.nn.softmax(jnp.stack(lses, axis=0), axis=0)
    return jnp.einsum('gbhs,gbhsd->bhsd', wts.astype(v.dtype), jnp.stack(outs, axis=0))


def memory_attention(qm, mem_n, w_mem_kv):
    km, vm = jnp.split(mem_n @ w_mem_kv, 2, axis=-1)
    km = split_heads(km, N_MEM_HEADS)
    vm = split_heads(vm, N_MEM_HEADS)
    sc = jnp.einsum('bhqd,bhkd->bhqk', qm, km, preferred_element_type=jnp.float32) * (HEAD_DIM ** -0.5)
    p = jax.nn.softmax(sc, axis=-1)
    return jnp.einsum('bhqk,bhkd->bhqd', p.astype(vm.dtype), vm)


def conv_ffn(xn, w_up, conv_w, conv_b, w_down):
    u = xn @ w_up
    s = u.shape[1]
    up = jnp.pad(u, ((0, 0), (CONV_WIDTH - 1, 0), (0, 0)))
    c = conv_b
    for j in range(CONV_WIDTH):
        c = c + conv_w[j] * up[:, j:j + s]
    val, gate = jnp.split(c, 2, axis=-1)
    return (jax.nn.silu(gate) * val) @ w_down


def setup_inputs(seed: int = 0) -> dict:
    key = jax.random.key(seed)
    ks = jax.random.split(key, 16)
    f32 = jnp.float32
    nrm = lambda k, shape, scale: jax.random.normal(k, shape, f32) * scale
    forget_bias_init = 3.0
    return {
        'x': nrm(ks[0], (BATCH, SEQ, D_MODEL), 1.0),
        'mem': nrm(ks[1], (BATCH, MEM_TOKENS, D_MODEL), 1.0),
        'norm_mix': 1.0 + nrm(ks[2], (DEPTH, D_MODEL), 0.1),
        'norm_mem': 1.0 + nrm(ks[3], (DEPTH, D_MODEL), 0.1),
        'norm_ffn': 1.0 + nrm(ks[4], (DEPTH, D_MODEL), 0.1),
        'w_in_fox': nrm(ks[5], (N_A_LAYERS, D_MODEL, FOX_IN), D_MODEL ** -0.5),
        'b_forget': forget_bias_init + nrm(ks[6], (N_A_LAYERS, N_MIX_HEADS), 0.5),
        'w_in_dil': nrm(ks[7], (N_B_LAYERS, D_MODEL, DIL_IN), D_MODEL ** -0.5),
        'w_mem_kv': nrm(ks[8], (DEPTH, D_MODEL, 2 * D_MEMQ), D_MODEL ** -0.5),
        'w_out': nrm(ks[9], (DEPTH, D_MODEL, D_MODEL), D_MODEL ** -0.5),
        'w_up': nrm(ks[10], (DEPTH, D_MODEL, 2 * D_FF), D_MODEL ** -0.5),
        'conv_w': nrm(ks[11], (DEPTH, CONV_WIDTH, 2 * D_FF), CONV_WIDTH ** -0.5),
        'conv_b': nrm(ks[12], (DEPTH, 2 * D_FF), 0.02),
        'w_down': nrm(ks[13], (DEPTH, D_FF, D_MODEL), D_FF ** -0.5),
        'norm_final': 1.0 + nrm(ks[14], (D_MODEL,), 0.1),
    }


def reference(x, mem, norm_mix, norm_mem, norm_ffn, w_in_fox, b_forget, w_in_dil,
              w_mem_kv, w_out, w_up, conv_w, conv_b, w_down, norm_final):
    s = x.shape[1]
    cos, sin = rope_tables(s)
    h = x
    for layer in range(DEPTH):
        kind = layer % N_MIXERS
        slot = layer // N_MIXERS
        xn = rmsnorm(h, norm_mix[layer])
        mn = rmsnorm(mem, norm_mem[layer])
        if kind == 0:
            proj = xn @ w_in_fox[slot]
            q, k, v, f_logit, qm = jnp.split(
                proj, [D_MIX, 2 * D_MIX, 3 * D_MIX, 3 * D_MIX + N_MIX_HEADS], axis=-1)
            log_f = jax.nn.log_sigmoid(
                (f_logit + b_forget[slot]).astype(jnp.float32)).transpose(0, 2, 1)
            mix = fox_attention(split_heads(q, N_MIX_HEADS), split_heads(k, N_MIX_HEADS),
                                split_heads(v, N_MIX_HEADS), log_f)
        else:
            proj = xn @ w_in_dil[slot]
            q, k, v, qm = jnp.split(proj, [D_MIX, 2 * D_MIX, 3 * D_MIX], axis=-1)
            qh = apply_rope(split_heads(q, N_MIX_HEADS), cos, sin)
            kh = apply_rope(split_heads(k, N_MIX_HEADS), cos, sin)
            mix = dilated_attention(qh, kh, split_heads(v, N_MIX_HEADS))
        mem_out = memory_attention(split_heads(qm, N_MEM_HEADS), mn, w_mem_kv[layer])
        heads = jnp.concatenate([mix, mem_out], axis=1)
        h = h + merge_heads(heads) @ w_out[layer]
        h = h + conv_ffn(rmsnorm(h, norm_ffn[layer]), w_up[layer], conv_w[layer],
                         conv_b[layer], w_down[layer])
    return rmsnorm(h, norm_final)
```

```python
import numpy as np
import ml_dtypes
import concourse.bass as bass
import concourse.mybir as mybir
from concourse.bass_utils import run_bass_kernel_spmd

F32 = mybir.dt.float32
BF16 = mybir.dt.bfloat16
AF = mybir.ActivationFunctionType
ALU = mybir.AluOpType

T = 2048
NT = 16
D = 1024
KC = 8
DFF = 2816
NJ = 22
XW = 2064
VW = 194
NEG = -30000.0
PAIRS = [[0, 1], [2, 3], [4, 5], [6, 7]]

SB_H = 16512
SB_C = SB_H + 65536
SB_X = SB_C + 9216
SB_Q = SB_X + 33024
SB_HD = SB_Q + 32768
SB_L = SB_HD + 32768
SB_END = 229344


def sl(start, count, step):
    return slice(start, start + (count - 1) * step + 1, step)


class Ctx:
    pass


class Phase:
    def __init__(self, cx):
        self.cx = cx
        self.ops = []

    def add(self, eng, fn, r=(), w=(), dma=False):
        self.ops.append([eng, fn, tuple(r), tuple(w), dma])

    def emit(self):
        cx = self.cx
        nc = cx.nc
        ops = self.ops
        lastw = {}
        readers = {}
        deps_all = []
        needed = set()
        for i, (eng, fn, r, w, dma) in enumerate(ops):
            raw = set()
            wxx = set()
            for b in r:
                if b in lastw:
                    raw.add(lastw[b])
            for b in w:
                if b in lastw:
                    wxx.add(lastw[b])
                wxx |= readers.get(b, set())
            raw.discard(i)
            wxx.discard(i)
            fd = set()
            for d in raw:
                de, _, _, _, ddma = ops[d]
                if (not ddma) and (not dma) and de == eng and eng == "pe":
                    continue
                fd.add(d)
            for d in wxx:
                de, _, _, _, ddma = ops[d]
                if (not ddma) and (not dma) and de == eng:
                    continue
                fd.add(d)
            deps_all.append(fd)
            needed |= fd
            for b in r:
                readers.setdefault(b, set()).add(i)
            for b in w:
                lastw[b] = i
                readers[b] = set()
        sig = {}
        streams = {e: [] for e in ("pe", "act", "dve", "pool", "sp")}
        dma_used = {}
        for i, (eng, fn, r, w, dma) in enumerate(ops):
            wmax = {}
            for d in deps_all[i]:
                sm, vl = sig[d]
                if sm.name not in wmax or wmax[sm.name][1] < vl:
                    wmax[sm.name] = (sm, vl)
            waits = list(wmax.values())
            if dma:
                q = cx.dmaq[eng]
                slot = q["n"] % len(q["sems"])
                q["n"] += 1
                sem = q["sems"][slot]
                if q["cnt"][slot] > 0:
                    waits.append((sem, 16 * q["cnt"][slot]))
                q["cnt"][slot] += 1
                sig[i] = (sem, 16 * q["cnt"][slot])
                dma_used[(eng, slot)] = sig[i]
                streams[eng].append((waits, fn, sem, 16))
            else:
                if i in needed:
                    cx.cnt[eng] += 1
                    sig[i] = (cx.sem[eng], cx.cnt[eng])
                    streams[eng].append((waits, fn, cx.sem[eng], 1))
                else:
                    streams[eng].append((waits, fn, None, 0))
        finals = {e: [] for e in streams}
        for (eng, slot), s in dma_used.items():
            finals[eng].append(s)

        def mk(eng):
            lst = streams[eng]
            fin = finals[eng]
            waited = cx.waited[eng]

            def body(e):
                for waits, fn, sem, inc in lst:
                    for (s, v) in waits:
                        if waited.get(s.name, 0) < v:
                            e.wait_ge(s, v)
                            waited[s.name] = v
                    ins = fn(e)
                    if sem is not None:
                        ins.then_inc(sem, inc)
                for (s, v) in fin:
                    if waited.get(s.name, 0) < v:
                        e.wait_ge(s, v)
                        waited[s.name] = v
            return body

        with nc.Block() as blk:
            decos = {"pe": blk.tensor, "act": blk.scalar, "dve": blk.vector, "pool": blk.gpsimd, "sp": blk.sync}
            for eng in ("sp", "pool", "pe", "act", "dve"):
                if streams[eng] or finals[eng]:
                    decos[eng](mk(eng))
        self.ops = []


class Bump:
    def __init__(self, nc, regions, tag):
        self.nc = nc
        self.regions = [list(r) for r in regions]
        self.tag = tag
        self.k = 0

    def t(self, name, shape, dt):
        size = int(np.prod(shape[1:])) * (4 if dt == F32 else 2)
        size = (size + 31) // 32 * 32
        for rg in self.regions:
            if rg[0] + size <= rg[1]:
                off = rg[0]
                rg[0] += size
                self.k += 1
                return self.nc.alloc_sbuf_tensor_at(f"{self.tag}_{name}_{self.k}", list(shape), dt, offset=off)
        raise RuntimeError(f"SBUF bump overflow {self.tag} {name} {shape}")


def build(nlayers=4, dbg=False, stop=None):
    nc = bass.Bass("TRN2", target_bir_lowering=False)
    cx = Ctx()
    cx.nc = nc

    def din(name, shape, dt=F32):
        return nc.dram_tensor(name, list(shape), dt, kind="ExternalInput").ap()

    x_d = din("x", [T, D])
    mem_d = din("mem", [256, D])
    win_fox = din("w_in_fox", [2, D, 2572])
    win_dil = din("w_in_dil", [2, D, 4096])
    wmem_d = din("w_mem_kv", [4, D, 512])
    wout_d = din("w_out", [4, D, D])
    wup_d = din("w_up", [4, D, 2 * DFF])
    wdown_d = din("w_down", [4, DFF, D])
    gains_d = din("gains", [128, 13 * 8])
    gfin_d = din("gfin", [1, D])
    convw_d = din("convw", [4, 128, 3 * 44])
    convb_d = din("convb", [4, 128, 44])
    bfor_d = din("b_forget", [2, 12])
    cbf_d = din("cbf", [128, 384], BF16)
    cf32_d = din("cf32", [128, 384])
    rope_d = din("rope", [2, 128, T])
    flags_d = din("flags", [128, 2])
    out_d = nc.dram_tensor("out", [T, D], F32, kind="ExternalOutput").ap()
    dbg_d = None
    if dbg:
        dbg_d = nc.dram_tensor("dbg", [nlayers, T, D], F32, kind="ExternalOutput").ap()

    kmine = [nc.dram_tensor(f"kmine{g}", [384, T], BF16) for g in range(2)]
    kall = [nc.dram_tensor(f"kall{g}", [768, T], BF16) for g in range(2)]
    vmine = [nc.dram_tensor(f"vmine{g}", [T, 2 * VW], BF16) for g in range(3)]
    vall = [nc.dram_tensor(f"vall{g}", [2 * T, 2 * VW], BF16) for g in range(3)]
    emine = nc.dram_tensor("emine", [T, 12], F32)
    eall = nc.dram_tensor("eall", [2 * T, 12], F32)
    hmine = nc.dram_tensor("hmine", [128, 16], BF16)
    hall = nc.dram_tensor("hall", [256, 16], BF16)

    cx.sem = {}
    cx.cnt = {}
    cx.waited = {}
    cx.dmaq = {}
    for e in ("pe", "act", "dve", "pool", "sp"):
        cx.sem[e] = nc.alloc_semaphore(f"s_{e}")
        cx.cnt[e] = 0
        cx.waited[e] = {}
    for e, n in (("sp", 8), ("pool", 6), ("act", 4)):
        cx.dmaq[e] = {"sems": [nc.alloc_semaphore(f"d_{e}{i}") for i in range(n)], "cnt": [0] * n, "n": 0}
    ccsems = [nc.alloc_semaphore(f"cc{i}") for i in range(8 * nlayers)]

    h = nc.alloc_sbuf_tensor_at("h", [128, NT, D], F32, offset=SB_H)
    cb = Bump(nc, [(SB_C, SB_X)], "c")
    cbf = cb.t("cbf", [128, 384], BF16)
    ident = cbf[:, 0:128]
    tri = cbf[:, 128:256]
    dmask = cbf[:, 128:384]
    cf32 = cb.t("cf32", [128, 384], F32)
    Umat = cf32[:, 0:128]
    ones = cf32[:, 128:256]
    sel127 = cf32[:, 256:384]
    gains = cb.t("gains", [128, 13, 8], F32)
    convw = cb.t("convw", [128, 3, 44], F32)
    convb = cb.t("convb", [128, 44], F32)
    bfor = cb.t("bfor", [128, 12], F32)
    flags = cb.t("flags", [128, 2], F32)
    pmask = flags[:, 0:1]
    hflag = flags[:, 1:2]
    ss = cb.t("ss", [128, 32], F32)
    rstd = cb.t("rstd", [128, 32], F32)
    kmT = cb.t("kmT", [128, 2, 256], BF16)
    vaugm = cb.t("vaugm", [128, 2, 2, VW], BF16)
    dl = cb.t("dl", [128, NT, 12], F32)
    dmid = cb.t("dmid", [128, NT, 12], F32)
    epre = cb.t("epre", [128, NT, 12], F32)
    zcol = cb.t("zcol", [128, 2], F32)
    halo = cb.t("halo", [128, 8, 2], BF16)
    epsc = cb.t("epsc", [128, 2], F32)
    xnT = nc.alloc_sbuf_tensor_at("xnT", [128, KC, XW], BF16, offset=SB_X)
    qT = nc.alloc_sbuf_tensor_at("qT", [128, 8, T], BF16, offset=SB_Q)
    headsT = nc.alloc_sbuf_tensor_at("headsT", [128, 8, T], BF16, offset=SB_HD)
    mnT = nc.alloc_sbuf_tensor_at("mnT", [128, KC, 256], BF16, offset=SB_HD)

    pb = [nc.alloc_psum_tensor(f"pb{i}", [128, 512], F32) for i in range(8)]

    def pbank(i):
        return pb[i][:, :]

    def pcols(c0, w):
        assert c0 // 512 == (c0 + w - 1) // 512
        return pb[c0 // 512][:, c0 % 512: c0 % 512 + w]

    def pbank_bf(i):
        return pbank(i).bitcast(BF16)

    ph = Phase(cx)
    uid = [0]

    def U():
        uid[0] += 1
        return uid[0]

    ph.add("sp", lambda e: e.dma_start(out=cbf[:, :], in_=cbf_d[:, :]), w=["cbf"], dma=True)
    ph.add("sp", lambda e: e.dma_start(out=cf32[:, :], in_=cf32_d[:, :]), w=["cf32"], dma=True)
    ph.add("sp", lambda e: e.dma_start(out=gains[:, :, :], in_=gains_d.rearrange("p (a c) -> p a c", c=8)), w=["gains"], dma=True)
    ph.add("sp", lambda e: e.dma_start(out=flags[:, :], in_=flags_d[:, :]), w=["flags"], dma=True)
    for i4 in range(4):
        ph.add("sp", lambda e, i4=i4: e.dma_start(
            out=h[:, i4 * 4:(i4 + 1) * 4, :],
            in_=x_d[i4 * 512:(i4 + 1) * 512, :].rearrange("(n p) d -> p n d", p=128)), w=[f"h{i4}"], dma=True)
    ph.add("dve", lambda e: e.memset(zcol[:, :], 0.0), w=["zcol"])
    ph.add("dve", lambda e: e.memset(epsc[:, :], 1e-6), w=["epsc"])
    ph.add("dve", lambda e: e.memset(xnT[:, :, 0:2], 0.0), w=["xhalo"])
    if nlayers < 2:
        dum = cb.t("dum", [128, 4], F32)
        ph.add("sp", lambda e: e.dma_start(out=dum[:, 0:2], in_=rope_d[0][:, 0:2]), w=["dum0"], dma=True)
        ph.add("sp", lambda e: e.dma_start(out=dum[:, 2:4], in_=win_dil[0][0:128, 0:2]), w=["dum1"], dma=True)
    ph.emit()

    def rmsnorm_tile(ph, src, gidx, dst, loc, name, sidx):
        k = U()
        junk = loc["junk"]
        xh = loc["xh"][k % 2]
        xhn = f"xh{k % 2}"
        ph.add("act", lambda e: e.activation(out=junk[:, :], in_=src, func=AF.Square, accum_out=ss[:, sidx:sidx + 1]),
               r=[name], w=["junk", f"ss{sidx}"])
        ph.add("act", lambda e: e.activation(out=rstd[:, sidx:sidx + 1], in_=ss[:, sidx:sidx + 1], func=AF.Sqrt, scale=1.0 / D, bias=epsc[:, 0:1]),
               r=[f"ss{sidx}", "epsc"], w=[f"rs{sidx}"])
        ph.add("dve", lambda e: e.reciprocal(out=rstd[:, sidx:sidx + 1], in_=rstd[:, sidx:sidx + 1]), r=[f"rs{sidx}"], w=[f"rs{sidx}"])
        ph.add("dve", lambda e: e.tensor_scalar(out=xh[:, :], in0=src, scalar1=rstd[:, sidx:sidx + 1], scalar2=None,
                                                op0=ALU.mult), r=[name, f"rs{sidx}"], w=[xhn])
        bk = 4 + (k % 2)
        pt = pbank_bf(bk)
        for c in range(KC):
            ph.add("pe", lambda e, c=c: e.transpose(pt[:, c * 128:(c + 1) * 128], xh[:, c * 128:(c + 1) * 128], ident),
                   r=[xhn, "cbf"], w=[f"pbk{bk}"])
        g = gains[:, gidx, :].unsqueeze(2).to_broadcast([128, 8, 128])
        ph.add("dve", lambda e: e.tensor_tensor(out=dst, in0=pt[:, :].rearrange("p (c t) -> p c t", c=8), in1=g, op=ALU.mult),
               r=[f"pbk{bk}", "gains"], w=[f"nt_{name}"])

    for l in range(nlayers):
        fox = (l % 2 == 0)
        slot = l // 2
        win = win_fox[slot] if fox else win_dil[slot]

        lb = Bump(nc, [(SB_L, SB_END)], f"p1_{l}")
        loc = {"junk": lb.t("junk", [128, D], BF16), "xh": [lb.t("xh0", [128, D], BF16), lb.t("xh1", [128, D], BF16)]}
        memt = lb.t("memt", [128, 2, D], F32)
        ph.add("sp", lambda e: e.dma_start(out=memt[:, :, :], in_=mem_d.rearrange("(n p) d -> p n d", p=128)), w=["memt"], dma=True)
        ph.add("sp", lambda e, l=l: e.dma_start(out=convw[:, :, :], in_=convw_d[l].rearrange("p (a c) -> p a c", c=44)), w=["convw"], dma=True)
        ph.add("sp", lambda e, l=l: e.dma_start(out=convb[:, :], in_=convb_d[l]), w=["convb"], dma=True)
        if fox:
            ph.add("sp", lambda e, slot=slot: e.dma_start(out=bfor[:, :], in_=bfor_d[slot].partition_broadcast(128)), w=["bfor"], dma=True)
        for i in range(NT):
            rmsnorm_tile(ph, h[:, i, :], 3 * l + 0, xnT[:, :, 2 + i * 128: 2 + (i + 1) * 128], loc, f"h{i // 4}", i)
        for i in range(2):
            rmsnorm_tile(ph, memt[:, i, :], 3 * l + 1, mnT[:, :, i * 128:(i + 1) * 128], loc, "memt", 16 + i)
        ph.emit()
        if stop == "p1":
            break

        lb = Bump(nc, [(SB_HD + 4096, SB_L), (SB_L, SB_END)], f"p2_{l}")
        wq = [lb.t(f"wq{i}", [128, KC, 128], BF16) for i in range(4)]
        wv = [lb.t(f"wv{i}", [128, KC, 396], BF16) for i in range(2)]
        vst = [lb.t(f"vst{i}", [128, 6, VW], BF16) for i in range(2)]
        kst = [lb.t(f"kst{i}", [128, 512], BF16) for i in range(3)]
        wmv = lb.t("wmv", [128, KC, 256], BF16)
        if fox:
            flog = lb.t("flog", [128, NT, 12], F32)
            spb = lb.t("spb", [128, NT, 12], F32)
            totb = lb.t("totb", [128, NT, 12], F32)
            offb = lb.t("offb", [128, NT, 12], F32)
            exdb = lb.t("exdb", [128, NT, 12], F32)
        else:
            cosF = lb.t("cosF", [128, T], F32)
            sinF = lb.t("sinF", [128, T], F32)
            rt = [lb.t(f"rt{i}", [128, 512], F32) for i in range(4)]
            ph.add("sp", lambda e: e.dma_start(out=cosF[:, :], in_=rope_d[0]), w=["cosF"], dma=True)
            ph.add("sp", lambda e: e.dma_start(out=sinF[:, :], in_=rope_d[1]), w=["sinF"], dma=True)
        for i in range(2):
            ph.add("dve", lambda e, i=i: e.memset(vst[i][:, :, :], 0.0), w=[f"vst{i}"])
            ph.add("dve", lambda e, i=i: e.memset(vst[i][:, :, 64:65], 1.0), w=[f"vst{i}"])
            ph.add("dve", lambda e, i=i: e.memset(vst[i][:, :, 97:98], 1.0), w=[f"vst{i}"])

        wsrc = win.rearrange("(kc p) n -> p kc n", p=128)
        vcol0 = 1536
        vw = [384, 396 if fox else 384]
        for hf in range(2):
            ph.add("pool", lambda e, hf=hf: e.dma_start(out=wv[hf][:, :, 0:vw[hf]], in_=wsrc[:, :, vcol0 + hf * 384: vcol0 + hf * 384 + vw[hf]]),
                   w=[f"wv{hf}"], dma=True)

        qmcol = 2316 if fox else 2304
        chunks = []
        for c in range(6):
            chunks.append(("q", c, c * 128))
        for c in range(6):
            chunks.append(("k", c, 768 + c * 128))
        for c in range(2):
            chunks.append(("m", c, qmcol + c * 128))
        wk = [0]
        pk = [0]

        def load_w(col, src=None):
            i = wk[0] % 4
            wk[0] += 1
            s = wsrc if src is None else src
            ph.add("pool", lambda e: e.dma_start(out=wq[i][:, :, :], in_=s[:, :, col:col + 128]), w=[f"wq{i}"], dma=True)
            return i

        def proj_fm(wi, rhs_fn, n, bank):
            for kc in range(KC):
                ph.add("pe", lambda e, kc=kc: e.matmul(pbank(bank)[:, 0:n], wq[wi][:, kc, :], rhs_fn(kc), start=(kc == 0), stop=(kc == KC - 1)),
                       r=[f"wq{wi}", "xnT", "mnT"], w=[f"pbk{bank}"])

        kstk = [0]
        for (kind, c, col) in chunks:
            rope_on = (not fox) and kind in ("q", "k")
            wi = load_w(col)
            if rope_on:
                wis = load_w(2560 + (0 if kind == "q" else 768) + c * 128)
            for tc in range(4):
                bank = pk[0] % 2
                pk[0] += 1
                proj_fm(wi, lambda kc, tc=tc: xnT[:, kc, 2 + tc * 512: 2 + (tc + 1) * 512], 512, bank)
                if rope_on:
                    proj_fm(wis, lambda kc, tc=tc: xnT[:, kc, 2 + tc * 512: 2 + (tc + 1) * 512], 512, bank + 2)
                if kind == "k":
                    ks = kstk[0] % 3
                    kstk[0] += 1
                    dst = kst[ks][:, :]
                    dname = f"kst{ks}"
                else:
                    cc = c if kind == "q" else 6 + c
                    dst = qT[:, cc, tc * 512:(tc + 1) * 512]
                    dname = f"qT{cc}"
                if rope_on:
                    r0 = rt[(pk[0] % 2) * 2]
                    r1 = rt[(pk[0] % 2) * 2 + 1]
                    n0 = f"rt{(pk[0] % 2) * 2}"
                    n1 = f"rt{(pk[0] % 2) * 2 + 1}"
                    ph.add("dve", lambda e, bank=bank, tc=tc, r0=r0: e.tensor_tensor(out=r0[:, :], in0=pbank(bank), in1=cosF[:, tc * 512:(tc + 1) * 512], op=ALU.mult),
                           r=[f"pbk{bank}", "cosF"], w=[n0])
                    ph.add("dve", lambda e, bank=bank, tc=tc, r1=r1: e.tensor_tensor(out=r1[:, :], in0=pbank(bank + 2), in1=sinF[:, tc * 512:(tc + 1) * 512], op=ALU.mult),
                           r=[f"pbk{bank + 2}", "sinF"], w=[n1])
                    ph.add("dve", lambda e, r0=r0, r1=r1, dst=dst: e.tensor_tensor(out=dst, in0=r0[:, :], in1=r1[:, :], op=ALU.add),
                           r=[n0, n1], w=[dname])
                else:
                    ph.add("act", lambda e, bank=bank, dst=dst: e.activation(out=dst, in_=pbank(bank), func=AF.Copy), r=[f"pbk{bank}"], w=[dname])
                if kind == "k":
                    ph.add("sp", lambda e, c=c, tc=tc, dst=dst: e.dma_start(out=kmine[c // 3][(c % 3) * 128:(c % 3 + 1) * 128, tc * 512:(tc + 1) * 512], in_=dst),
                           r=[dname], w=["kmine"], dma=True)

        wmsrc = wmem_d[l].rearrange("(kc p) n -> p kc n", p=128)
        for c in range(2):
            wi = load_w(c * 128, src=wmsrc)
            bank = pk[0] % 2
            pk[0] += 1
            proj_fm(wi, lambda kc: mnT[:, kc, :], 256, bank)
            ph.add("act", lambda e, bank=bank, c=c: e.activation(out=kmT[:, c, :], in_=pbank(bank)[:, 0:256], func=AF.Copy), r=[f"pbk{bank}"], w=["kmT"])
        ph.add("pool", lambda e: e.dma_start(out=wmv[:, :, :], in_=wmsrc[:, :, 256:512]), w=["wmv"], dma=True)
        ph.add("dve", lambda e: e.memset(vaugm[:, :, :, :], 0.0), w=["vaugm"])
        ph.add("dve", lambda e: e.memset(vaugm[:, :, :, 64:65], 1.0), w=["vaugm"])
        ph.add("dve", lambda e: e.memset(vaugm[:, :, :, 97:98], 1.0), w=["vaugm"])
        for i in range(2):
            bank = pk[0] % 2
            pk[0] += 1
            for kc in range(KC):
                ph.add("pe", lambda e, kc=kc, i=i, bank=bank: e.matmul(pbank(bank)[:, 0:256], mnT[:, kc, i * 128:(i + 1) * 128], wmv[:, kc, :],
                                                                     start=(kc == 0), stop=(kc == KC - 1)), r=["mnT", "wmv"], w=[f"pbk{bank}"])
            pv = pbank(bank)[:, 0:256].rearrange("p (a b d) -> p a b d", a=2, b=2)
            ph.add("act", lambda e, i=i, pv=pv: e.activation(out=vaugm[:, i, :, 0:64], in_=pv[:, :, 0, :], func=AF.Copy), r=[f"pbk{bank}"], w=["vaugm"])
            ph.add("act", lambda e, i=i, pv=pv: e.activation(out=vaugm[:, i, :, 129:193], in_=pv[:, :, 1, :], func=AF.Copy), r=[f"pbk{bank}"], w=["vaugm"])

        for i in range(NT):
            vs = vst[i % 2]
            vsn = f"vst{i % 2}"
            for hf in range(2):
                bank = 2 + (pk[0] % 2)
                pk[0] += 1
                n = vw[hf]
                for kc in range(KC):
                    ph.add("pe", lambda e, kc=kc, i=i, hf=hf, bank=bank, n=n: e.matmul(pbank(bank)[:, 0:n], xnT[:, kc, 2 + i * 128: 2 + (i + 1) * 128], wv[hf][:, kc, 0:n],
                                                                                  start=(kc == 0), stop=(kc == KC - 1)), r=["xnT", f"wv{hf}"], w=[f"pbk{bank}"])
                pv = pbank(bank)[:, 0:384].rearrange("p (a b d) -> p a b d", a=3, b=2)
                ph.add("act", lambda e, vs=vs, hf=hf, pv=pv: e.activation(out=vs[:, hf * 3:(hf + 1) * 3, 0:64], in_=pv[:, :, 0, :], func=AF.Copy),
                       r=[f"pbk{bank}"], w=[vsn])
                ph.add("dve", lambda e, vs=vs, hf=hf, pv=pv: e.tensor_copy(out=vs[:, hf * 3:(hf + 1) * 3, 129:193], in_=pv[:, :, 1, :]),
                       r=[f"pbk{bank}"], w=[vsn])
                if fox and hf == 1:
                    ph.add("dve", lambda e, i=i, bank=bank: e.tensor_tensor(out=flog[:, i, :], in0=pbank(bank)[:, 384:396], in1=bfor[:, :], op=ALU.add),
                           r=[f"pbk{bank}", "bfor"], w=["flog"])
            for g in range(3):
                ph.add("sp", lambda e, i=i, vs=vs, g=g: e.dma_start(out=vmine[g][i * 128:(i + 1) * 128, :], in_=vs[:, 2 * g:2 * g + 2, :].rearrange("p a b -> p (a b)")),
                       r=[vsn], w=["vmine"], dma=True)

        if fox:
            fl2 = flog[:, :, :].rearrange("p a b -> p (a b)")
            sp2 = spb[:, :, :].rearrange("p a b -> p (a b)")
            ph.add("act", lambda e: e.activation(out=sp2, in_=fl2, func=AF.Exp, scale=-1.0), r=["flog"], w=["spb"])
            ph.add("act", lambda e: e.activation(out=sp2, in_=sp2, func=AF.Ln, bias=1.0), r=["spb"], w=["spb"])
            ph.add("pe", lambda e: e.matmul(pbank(0)[:, 0:192], Umat, sp2, start=True, stop=True), r=["spb", "cf32"], w=["pbk0"])
            ph.add("pe", lambda e: e.matmul(pbank(1)[:, 0:192], ones, sp2, start=True, stop=True), r=["spb", "cf32"], w=["pbk1"])
            ph.add("dve", lambda e: e.tensor_copy(out=totb[:, :, :].rearrange("p a b -> p (a b)"), in_=pbank(1)[:, 0:192]), r=["pbk1"], w=["totb"])
            ph.add("dve", lambda e: e.memset(offb[:, 0, :], 0.0), w=["offb"])
            for i in range(1, NT):
                ph.add("dve", lambda e, i=i: e.tensor_tensor(out=offb[:, i, :], in0=offb[:, i - 1, :], in1=totb[:, i - 1, :], op=ALU.add),
                       r=["offb", "totb"], w=["offb"])
            ph.add("dve", lambda e: e.tensor_tensor(out=dl[:, :, :].rearrange("p a b -> p (a b)"), in0=pbank(0)[:, 0:192],
                                                    in1=offb[:, :, :].rearrange("p a b -> p (a b)"), op=ALU.add), r=["pbk0", "offb"], w=["dl"])
            ph.add("pe", lambda e: e.matmul(pbank(2)[:, 0:192], sel127, dl[:, :, :].rearrange("p a b -> p (a b)"), start=True, stop=True),
                   r=["dl", "cf32"], w=["pbk2"])
            ph.add("dve", lambda e: e.tensor_copy(out=dmid[:, :, :].rearrange("p a b -> p (a b)"), in_=pbank(2)[:, 0:192]), r=["pbk2"], w=["dmid"])
            dtot = dmid[:, 15:16, :].to_broadcast([128, NT, 12])
            ph.add("dve", lambda e: e.tensor_tensor(out=exdb[:, :, :], in0=dl[:, :, :], in1=dtot, op=ALU.subtract), r=["dl", "dmid"], w=["exdb"])
            ph.add("sp", lambda e: e.dma_start(out=emine.ap().rearrange("(i p) h -> p i h", p=128), in_=exdb[:, :, :]), r=["exdb"], w=["emine"], dma=True)
        ph.emit()
        if stop == "p2":
            break

        cs = ccsems[8 * l: 8 * l + 8]

        def v2(t, a):
            return t.ap().rearrange("(a b) c -> a (b c)", a=a)

        def cc_body(e, cs=cs, fox=fox):
            for g in range(2):
                e.collective_compute("AllGather", ALU.bypass, replica_groups=PAIRS, ins=[v2(kmine[g], 128)], outs=[v2(kall[g], 256)]).then_inc(cs[g])
                e.wait_ge(cs[g], 1)
            for g in range(3):
                e.collective_compute("AllGather", ALU.bypass, replica_groups=PAIRS, ins=[v2(vmine[g], 128)], outs=[v2(vall[g], 256)]).then_inc(cs[2 + g])
                e.wait_ge(cs[2 + g], 1)
            if fox:
                e.collective_compute("AllGather", ALU.bypass, replica_groups=PAIRS, ins=[v2(emine, 128)], outs=[v2(eall, 256)]).then_inc(cs[5])
                e.wait_ge(cs[5], 1)
        with nc.Block() as blk:
            blk.gpsimd(cc_body)
        if stop == "cc1":
            break

        lbx = Bump(nc, [(SB_X, SB_Q)], f"p3x_{l}")
        lbl = Bump(nc, [(SB_L, SB_END)], f"p3l_{l}")
        kTb = [lbx.t(f"kT{i}", [128, 2 * T], BF16) for i in range(2)]
        nfb = 2 if fox else 1
        bcs = [(lbx if fox else lbl).t(f"bcs{i}", [128, 512], F32) for i in range(nfb)]
        rden = [(lbx if fox else lbl).t(f"rden{i}", [128, 512], F32) for i in range(nfb)]
        fin_k = [0]

        class Pipe:
            def __init__(self):
                self.q = []

            def defer(self, n, fn):
                self.q.append([n, fn])

            def tick(self):
                for it in self.q:
                    it[0] -= 1
                ready = [it for it in self.q if it[0] <= 0]
                self.q = [it for it in self.q if it[0] > 0]
                for it in ready:
                    it[1]()

            def flush(self):
                while self.q:
                    self.tick()

        pipe = Pipe()
        SK = 3 if fox else 2
        SK_DIL = 2
        DIL_POOL_ONLY = True

        def finalize(acc_ap, accname, odd, dst, dname, sbuf_acc=False, delay=4):
            k = fin_k[0] % nfb
            fb = 6 + (fin_k[0] % 2) if fox else 7
            fin_k[0] += 1
            dp = 32 if odd else 64
            r0 = 64 if odd else 0
            if sbuf_acc:
                rd = acc_ap[dp:dp + 1, :]
                rdn = accname
            else:
                rd = rden[k][dp:dp + 1, :]
                rdn = f"rden{k}"
            ph.add("dve", lambda e: e.reciprocal(out=rd, in_=acc_ap[dp:dp + 1, :]), r=[accname], w=[rdn])

            def stage_b():
                ph.add("pe", lambda e: e.matmul(pbank(fb), ones[dp:dp + 1, :], rd, start=True, stop=True), r=[rdn, "cf32"], w=[f"pbk{fb}"])
                if fox:
                    ph.add("dve", lambda e: e.tensor_copy(out=bcs[k][r0:r0 + 64, :], in_=pbank(fb)[r0:r0 + 64, :]), r=[f"pbk{fb}"], w=[f"bcs{k}"])
                else:
                    ph.add("act", lambda e: e.activation(out=bcs[k][r0:r0 + 64, :], in_=pbank(fb)[r0:r0 + 64, :], func=AF.Copy), r=[f"pbk{fb}"], w=[f"bcs{k}"])
                ph.add("dve", lambda e: e.tensor_tensor(out=dst, in0=acc_ap[r0:r0 + 64, :], in1=bcs[k][r0:r0 + 64, :], op=ALU.mult),
                       r=[accname, f"bcs{k}"], w=[dname])
            stage_b()

        def vcols(odd):
            return (65, 193) if odd else (0, 65)

        acck = [0]
        sk = [0]
        ptk = [0]

        if fox:
            vgb = [lbl.t(f"vg{i}", [128, 32, VW], BF16) for i in range(2)]
            pT = [lbl.t(f"pT{i}", [128, 512], BF16) for i in range(6)]
            bt = [lbx.t(f"bt{i}", [128, 8, 32], F32) for i in range(4)]
            ph.add("sp", lambda e: e.dma_start(out=epre[:, :, :], in_=eall.ap()[0:T, :].rearrange("(i p) h -> p i h", p=128)), w=["epre"], dma=True)
            ph.add("dve", lambda e: e.tensor_scalar(out=epre[:, :, :], in0=epre[:, :, :], scalar1=pmask, scalar2=None, op0=ALU.add), r=["epre", "flags"], w=["epre"])
            btk = [0]
            for j in range(6):
                kb = kTb[j % 2]
                kn = f"kT{j % 2}"
                vg = vgb[j % 2]
                vn = f"vg{j % 2}"
                ph.add("sp", lambda e, j=j, kb=kb: e.dma_start(out=kb[:, 0:T], in_=kall[j // 3].ap()[(j % 3) * 128:(j % 3 + 1) * 128, :]), w=[kn], dma=True)
                ph.add("sp", lambda e, j=j, kb=kb: e.dma_start(out=kb[:, T:2 * T], in_=kmine[j // 3].ap()[(j % 3) * 128:(j % 3 + 1) * 128, :]), w=[kn], dma=True)
                for q4 in range(4):
                    ph.add("sp", lambda e, j=j, vg=vg, q4=q4: e.dma_start(out=vg[:, q4 * 4:(q4 + 1) * 4, :],
                           in_=vall[j // 2].ap()[q4 * 512:(q4 + 1) * 512, (j % 2) * VW:(j % 2 + 1) * VW].rearrange("(i p) w -> p i w", p=128)), w=[vn], dma=True)
                    ph.add("sp", lambda e, j=j, vg=vg, q4=q4: e.dma_start(out=vg[:, 16 + q4 * 4:16 + (q4 + 1) * 4, :],
                           in_=vmine[j // 2].ap()[q4 * 512:(q4 + 1) * 512, (j % 2) * VW:(j % 2 + 1) * VW].rearrange("(i p) w -> p i w", p=128)), w=[vn], dma=True)
                for hh in range(2):
                    hd = 2 * j + hh
                    odd = (hh == 1)
                    r0 = 64 * hh
                    bi = btk[0] % 4
                    btk[0] += 1
                    btt = bt[bi]
                    btn = f"bt{bi}"
                    for m in range(8):
                        ph.add("dve", lambda e, m=m, hd=hd, btt=btt: e.tensor_scalar(out=btt[:, m, 0:16], in0=epre[:, :, hd], scalar1=dmid[:, 2 * m, hd:hd + 1],
                                                                                   scalar2=None, op0=ALU.subtract), r=["epre", "dmid"], w=[btn + f"a{m}"])
                        ph.add("dve", lambda e, m=m, hd=hd, btt=btt: e.tensor_scalar(out=btt[:, m, 16:32], in0=dl[:, :, hd], scalar1=dmid[:, 2 * m, hd:hd + 1],
                                                                                   scalar2=None, op0=ALU.subtract), r=["dl", "dmid"], w=[btn + f"b{m}"])
                    v0, v1 = vcols(odd)
                    M = 128 if odd else 65
                    for qc in range(4):
                        ab = acck[0] % 2
                        acck[0] += 1
                        accn = f"pbk{ab}"
                        diag = [16 + 4 * qc + kj for kj in range(4)]
                        full = list(range(1, 16)) + list(range(16, 16 + 4 * qc))
                        order = [0] + diag + full
                        for idx, kt in enumerate(order):
                            kj = kt - (16 + 4 * qc) if kt in diag else 0
                            c0 = kj * 128
                            sb = 2 + (sk[0] % 4)
                            sk[0] += 1
                            ph.add("pe", lambda e, kb=kb, kt=kt, c0=c0, sb=sb, qc=qc, j=j, r0=r0: e.matmul(
                                pbank(sb)[:, c0:512], kb[r0:r0 + 64, kt * 128:(kt + 1) * 128], qT[r0:r0 + 64, j, qc * 512 + c0:(qc + 1) * 512],
                                start=True, stop=True), r=[kn, f"qT{j}"], w=[f"pbk{sb}"])
                            pi = ptk[0] % 6
                            ptk[0] += 1
                            pt_ = pT[pi]
                            ptn = f"pT{pi}"
                            rn = []
                            for hf in range(2):
                                a0 = max(c0, hf * 256)
                                a1 = (hf + 1) * 256
                                if a0 >= a1:
                                    continue
                                m = 2 * qc + hf
                                src_b = btn + (f"a{m}" if kt < 16 else f"b{m}")
                                ph.add("act", lambda e, sb=sb, a0=a0, a1=a1, m=m, kt=kt, pt_=pt_, btt=btt: e.activation(
                                    out=pt_[:, a0:a1], in_=pbank(sb)[:, a0:a1], func=AF.Exp, scale=0.125, bias=btt[:, m, kt:kt + 1]),
                                    r=[f"pbk{sb}", src_b], w=[ptn + f"h{hf}"])
                                rn.append(ptn + f"h{hf}")
                            if kt in diag:
                                hfd = c0 // 256
                                ph.add("pool", lambda e, pt_=pt_, c0=c0: e.tensor_tensor(out=pt_[:, c0:c0 + 128], in0=pt_[:, c0:c0 + 128], in1=tri, op=ALU.mult),
                                       r=[ptn + f"h{hfd}", "cbf"], w=[ptn + f"h{hfd}"])
                            last = (idx == len(order) - 1)

                            def pv(vg=vg, kt=kt, v0=v0, v1=v1, pt_=pt_, c0=c0, ab=ab, idx=idx, M=M, last=last, rn=tuple(rn), vn=vn, accn=accn,
                                   odd=odd, r0=r0, j=j, qc=qc):
                                ph.add("pe", lambda e: e.matmul(pbank(ab)[0:M, c0:512], vg[:, kt, v0:v1], pt_[:, c0:512], start=(idx == 0), stop=last),
                                       r=[vn] + list(rn), w=[accn])
                                if last:
                                    finalize(pbank(ab), accn, odd, headsT[r0:r0 + 64, j, qc * 512:(qc + 1) * 512], f"hT{j}")
                            pipe.defer(SK, pv)
                            pipe.tick()
        else:
            pipe.flush()
            accs = [lbx.t(f"accs{i}", [128, T], F32) for i in range(2)]
            pT = [lbl.t(f"pT{i}", [128, 256], BF16) for i in range(8)]
            BR = [(1, 17), (4, 20), (16, 32)]
            ntl = 17 + 20 + 32
            vgd = lbl.t("vgd", [128, ntl, VW], BF16)
            mk_eng = [0]
            for j in range(6):
                kb = kTb[j % 2]
                kn = f"kT{j % 2}"
                ph.add("sp", lambda e, j=j, kb=kb: e.dma_start(out=kb[:, 0:T], in_=kall[j // 3].ap()[(j % 3) * 128:(j % 3 + 1) * 128, :]), w=[kn], dma=True)
                ph.add("sp", lambda e, j=j, kb=kb: e.dma_start(out=kb[:, T:2 * T], in_=kmine[j // 3].ap()[(j % 3) * 128:(j % 3 + 1) * 128, :]), w=[kn], dma=True)
                tbase = {}
                ti = 0
                for (d, _) in BR:
                    nbo = 16 // d
                    for r in range(d):
                        tbase[(d, r)] = ti
                        p0 = (nbo - 1) * 128 * d + r
                        src = vall[j // 2].ap()[sl(p0, 128, d), (j % 2) * VW:(j % 2 + 1) * VW] if d > 1 else vall[j // 2].ap()[p0:p0 + 128, (j % 2) * VW:(j % 2 + 1) * VW]
                        ph.add("sp", lambda e, src=src, ti=ti: e.dma_start(out=vgd[:, ti, :], in_=src), w=[f"vgd{d}"], dma=True)
                        if d > 1:
                            src2 = vmine[j // 2].ap()[r: T: d, (j % 2) * VW:(j % 2 + 1) * VW].rearrange("(n p) w -> p n w", p=128)
                        else:
                            src2 = vmine[j // 2].ap()[:, (j % 2) * VW:(j % 2 + 1) * VW].rearrange("(n p) w -> p n w", p=128)
                        if nbo >= 8:
                            for s4 in range(nbo // 4):
                                ph.add("sp", lambda e, src2=src2, ti=ti, s4=s4: e.dma_start(out=vgd[:, ti + 1 + s4 * 4: ti + 1 + (s4 + 1) * 4, :], in_=src2[:, s4 * 4:(s4 + 1) * 4, :]),
                                       w=[f"vgd{d}"], dma=True)
                        else:
                            ph.add("sp", lambda e, src2=src2, ti=ti, nbo=nbo: e.dma_start(out=vgd[:, ti + 1: ti + 1 + nbo, :], in_=src2), w=[f"vgd{d}"], dma=True)
                        ti += 1 + nbo
                for hh in range(2):
                    odd = (hh == 1)
                    r0 = 64 * hh
                    v0, v1 = vcols(odd)
                    M = 128 if odd else 65
                    acs = accs[hh]
                    acn = f"accs{hh}"
                    for bi_, (d, _) in enumerate(BR):
                        nbo = 16 // d
                        units = [(r, kk) for r in range(d) for kk in range(nbo + 1)]
                        for ui, (r, kk) in enumerate(units):
                            tb = tbase[(d, r)]
                            if kk == 0:
                                kpos0 = (nbo - 1) * 128 * d + r
                            else:
                                kpos0 = T + (kk - 1) * 128 * d + r
                            qb0 = max(kk - 1, 0)
                            qb1 = min(kk, nbo - 1)
                            nq = (qb1 - qb0 + 1) * 128
                            qpos0 = qb0 * 128 * d + r
                            sb = 4 + (sk[0] % 3)
                            sk[0] += 1
                            ph.add("pe", lambda e, kb=kb, kpos0=kpos0, d=d, sb=sb, qpos0=qpos0, nq=nq, j=j, r0=r0: e.matmul(
                                pbank(sb)[:, 0:nq], kb[r0:r0 + 64, sl(kpos0, 128, d)], qT[r0:r0 + 64, j, sl(qpos0, nq, d)],
                                start=True, stop=True), r=[kn, f"qT{j}"], w=[f"pbk{sb}"])
                            pi = ptk[0] % 8
                            ptk[0] += 1
                            pt_ = pT[pi]
                            ptn = f"pT{pi}"
                            bias_ap = pmask if kk == 0 else zcol[:, 0:1]
                            ph.add("act", lambda e, sb=sb, nq=nq, pt_=pt_, bias_ap=bias_ap: e.activation(
                                out=pt_[:, 0:nq], in_=pbank(sb)[:, 0:nq], func=AF.Exp, scale=0.125, bias=bias_ap), r=[f"pbk{sb}", "flags", "zcol"], w=[ptn])
                            if kk == 0:
                                msk = dmask[:, 128:256]
                            elif kk == nbo:
                                msk = dmask[:, 0:128]
                            else:
                                msk = dmask[:, 0:256]
                            meng = "dve" if (mk_eng[0] % 4 == 3) else "pool"
                            mk_eng[0] += 1
                            ph.add(meng, lambda e, pt_=pt_, nq=nq, msk=msk: e.tensor_tensor(out=pt_[:, 0:nq], in0=pt_[:, 0:nq], in1=msk, op=ALU.mult),
                                   r=[ptn, "cbf"], w=[ptn])
                            lastu = (ui == len(units) - 1)

                            def pv(kk=kk, tb=tb, v0=v0, v1=v1, pt_=pt_, ptn=ptn, qb0=qb0, qb1=qb1, r=r, nbo=nbo, M=M, lastu=lastu, d=d, bi_=bi_,
                                   acs=acs, acn=acn, odd=odd, r0=r0, j=j):
                                for qi, qb in enumerate(range(qb0, qb1 + 1)):
                                    same = (qb == kk - 1)
                                    col = (r * nbo + qb) * 128
                                    bnk = col // 512
                                    ph.add("pe", lambda e, qi=qi, col=col, same=same: e.matmul(
                                        pcols(col, 128)[0:M, :], vgd[:, tb + kk, v0:v1], pt_[:, qi * 128:(qi + 1) * 128], start=(not same), stop=same),
                                        r=[f"vgd{d}", ptn], w=[f"pbk{bnk}"])
                                if lastu:
                                    wdt = T // d
                                    for rr in range(d):
                                        for s in range(max(1, wdt // 512)):
                                            w_ = min(wdt, 512)
                                            c_src = rr * wdt + s * w_
                                            src = pcols(c_src, w_)
                                            if d == 1:
                                                dsta = acs[:, s * 512:(s + 1) * 512]
                                            else:
                                                dsta = acs[:, sl(rr + s * w_ * d, w_, d)]
                                            bnk = c_src // 512
                                            if bi_ == 0:
                                                ph.add("act", lambda e, src=src, dsta=dsta: e.activation(out=dsta, in_=src, func=AF.Copy), r=[f"pbk{bnk}"], w=[acn + f"s{s}"])
                                            else:
                                                ph.add("dve", lambda e, src=src, dsta=dsta: e.tensor_tensor(out=dsta, in0=src, in1=dsta, op=ALU.add),
                                                       r=[f"pbk{bnk}"] + [acn + f"s{x}" for x in range(4)], w=[acn + f"s{x}" for x in range(4)])
                                    if bi_ == 2:
                                        for qc in range(4):
                                            finalize(acs[:, qc * 512:(qc + 1) * 512], acn + f"s{qc}", odd, headsT[r0:r0 + 64, j, qc * 512:(qc + 1) * 512], f"hT{j}", sbuf_acc=True)
                            pipe.defer(SK_DIL, pv)
                            pipe.tick()
                pipe.flush()

        if True:
            pTm = [lbl.t(f"pTm{i}", [128, 512], BF16) for i in range(4)]
            mk = [0]
            nsb = 4 if fox else 3
            sb0 = 2 if fox else 4
            for j in range(2):
                for hh in range(2):
                    odd = (hh == 1)
                    r0 = 64 * hh
                    v0, v1 = vcols(odd)
                    M = 128 if odd else 65
                    for qc in range(4):
                        ab = acck[0] % 2
                        acck[0] += 1
                        for kt in range(2):
                            sb = sb0 + (sk[0] % nsb)
                            sk[0] += 1
                            pi = mk[0] % 4
                            mk[0] += 1
                            ph.add("pe", lambda e, j=j, kt=kt, sb=sb, qc=qc, r0=r0: e.matmul(
                                pbank(sb), kmT[r0:r0 + 64, j, kt * 128:(kt + 1) * 128], qT[r0:r0 + 64, 6 + j, qc * 512:(qc + 1) * 512], start=True, stop=True),
                                r=["kmT", f"qT{6 + j}"], w=[f"pbk{sb}"])
                            ph.add("act", lambda e, sb=sb, pi=pi: e.activation(out=pTm[pi][:, :], in_=pbank(sb), func=AF.Exp, scale=0.125, bias=zcol[:, 0:1]),
                                   r=[f"pbk{sb}", "zcol"], w=[f"pTm{pi}"])

                            def pvm(j=j, kt=kt, v0=v0, v1=v1, pi=pi, ab=ab, M=M, odd=odd, r0=r0, qc=qc):
                                ph.add("pe", lambda e: e.matmul(pbank(ab)[0:M, :], vaugm[:, kt, j, v0:v1], pTm[pi][:, :], start=(kt == 0), stop=(kt == 1)),
                                       r=["vaugm", f"pTm{pi}"], w=[f"pbk{ab}"])
                                if kt == 1:
                                    finalize(pbank(ab), f"pbk{ab}", odd, headsT[r0:r0 + 64, 6 + j, qc * 512:(qc + 1) * 512], f"hT{6 + j}", delay=2)
                            pipe.defer(SK, pvm)
                            pipe.tick()
        pipe.flush()
        ph.emit()
        if stop == "p3":
            break

        lb = Bump(nc, [(SB_Q, SB_HD), (SB_L, SB_END)], f"p4_{l}")
        wo = lb.t("wo", [128, KC, D], BF16)
        loc = {"junk": lb.t("junk", [128, D], BF16), "xh": [lb.t("xh0", [128, D], BF16), lb.t("xh1", [128, D], BF16)]}
        wosrc = wout_d[l].rearrange("(kc p) n -> p kc n", p=128)
        for kc in range(KC):
            ph.add("pool", lambda e, kc=kc: e.dma_start(out=wo[:, kc, :], in_=wosrc[:, kc, :]), w=[f"wo{kc}"], dma=True)
        ok = [0]

        def outproj_tile(i):
            for hf in range(2):
                bank = ok[0] % 2
                ok[0] += 1
                for kc in range(KC):
                    ph.add("pe", lambda e, kc=kc, i=i, hf=hf, bank=bank: e.matmul(pbank(bank), headsT[:, kc, i * 128:(i + 1) * 128], wo[:, kc, hf * 512:(hf + 1) * 512],
                                                                                  start=(kc == 0), stop=(kc == KC - 1)), r=[f"hT{kc}", f"wo{kc}"], w=[f"pbk{bank}"])
                ph.add("dve", lambda e, i=i, hf=hf, bank=bank: e.tensor_tensor(out=h[:, i, hf * 512:(hf + 1) * 512], in0=pbank(bank), in1=h[:, i, hf * 512:(hf + 1) * 512], op=ALU.add),
                       r=[f"pbk{bank}", f"ht{i}"], w=[f"ht{i}"])

        def norm2_tile(i):
            rmsnorm_tile(ph, h[:, i, :], 3 * l + 2, xnT[:, :, 2 + i * 128: 2 + (i + 1) * 128], loc, f"ht{i}", i)

        outproj_tile(NT - 1)
        norm2_tile(NT - 1)
        ph.add("sp", lambda e: e.dma_start(out=hmine.ap().rearrange("p (c t) -> p c t", t=2), in_=xnT[:, :, 2 + T - 2: 2 + T]), r=[f"nt_ht{NT - 1}"], w=["hmine"], dma=True)
        ph.emit()
        cs3 = ccsems[8 * l + 6]

        def cc2_body(e, cs3=cs3):
            e.collective_compute("AllGather", ALU.bypass, replica_groups=PAIRS, ins=[hmine.ap().opt()], outs=[hall.ap().opt()]).then_inc(cs3)
            e.wait_ge(cs3, 1)
        with nc.Block() as blk:
            blk.gpsimd(cc2_body)
        ph.add("sp", lambda e: e.dma_start(out=halo[:, :, :], in_=hall.ap()[0:128, :].rearrange("p (c t) -> p c t", t=2)), w=["halo"], dma=True)
        ph.add("dve", lambda e: e.tensor_scalar(out=xnT[:, :, 0:2], in0=halo[:, :, :], scalar1=hflag, scalar2=None, op0=ALU.mult), r=["halo", "flags"], w=["xhalo"])
        for i in range(NT - 1):
            outproj_tile(i)
            if i >= 1:
                norm2_tile(i - 1)
        norm2_tile(NT - 2)
        ph.emit()

        if stop == "p4":
            break
        wd = nc.alloc_sbuf_tensor_at(f"wd_{l}", [128, NJ, D], BF16, offset=SB_Q)
        gT = nc.alloc_sbuf_tensor_at(f"gT_{l}", [128, NJ, 1024], BF16, offset=SB_Q + 45056)
        lb = Bump(nc, [(SB_Q + 2 * 45056, SB_END)], f"p5_{l}")
        wu = [lb.t(f"wu{i}", [128, KC, 2, 128], BF16) for i in range(2)]
        cv = [lb.t(f"cv{i}", [128, 256], F32) for i in range(4)]
        sg = [lb.t(f"sg{i}", [128, 256], F32) for i in range(2)]
        wdsrc = wdown_d[l].rearrange("(j p) n -> p j n", p=128)
        pipe5 = Pipe()
        wusrc = wup_d[l].rearrange("(kc p) n -> p kc n", p=128)
        for tcb in range(2):
            for j in range(NJ):
                if tcb == 0 and j % 2 == 0:
                    ph.add("pool", lambda e, j=j: e.dma_start(out=wd[:, j:j + 2, :], in_=wdsrc[:, j:j + 2, :]), w=[f"wd{j}", f"wd{j + 1}"], dma=True)
                wi = (tcb * NJ + j) % 2
                ph.add("pool", lambda e, j=j, wi=wi: e.dma_start(out=wu[wi][:, :, 0, :], in_=wusrc[:, :, j * 128:(j + 1) * 128]), w=[f"wu{wi}"], dma=True)
                ph.add("pool", lambda e, j=j, wi=wi: e.dma_start(out=wu[wi][:, :, 1, :], in_=wusrc[:, :, DFF + j * 128: DFF + (j + 1) * 128]), w=[f"wu{wi}"], dma=True)
                for sub in range(4):
                    k = U()
                    tok0 = tcb * 1024 + sub * 256
                    pv_b = 2 + (k % 3) * 2
                    pg_b = pv_b + 1
                    for (vg_, bnk) in ((0, pv_b), (1, pg_b)):
                        for kc in range(KC):
                            ph.add("pe", lambda e, kc=kc, vg_=vg_, bnk=bnk, wi=wi, tok0=tok0: e.matmul(pbank(bnk)[:, 0:258], wu[wi][:, kc, vg_, :], xnT[:, kc, tok0:tok0 + 258],
                                                                                             start=(kc == 0), stop=(kc == KC - 1)), r=[f"wu{wi}", "xnT", "xhalo"], w=[f"pbk{bnk}"])
                    cvv = cv[(k % 2) * 2]
                    cvg = cv[(k % 2) * 2 + 1]
                    nv = f"cv{(k % 2) * 2}"
                    ng = f"cv{(k % 2) * 2 + 1}"
                    sgb = sg[k % 2]
                    sgn = f"sg{k % 2}"
                    vgl = ((0, pv_b, cvv, nv), (1, pg_b, cvg, ng))
                    for (vg_, bnk, cbuf, cn) in vgl:
                        jj = j + vg_ * NJ
                        ps = pbank(bnk)
                        ph.add("act", lambda e, ps=ps, cbuf=cbuf, jj=jj: e.activation(out=cbuf[:, :], in_=ps[:, 2:258], func=AF.Identity,
                                                                                      scale=convw[:, 2, jj:jj + 1], bias=convb[:, jj:jj + 1]),
                               r=[f"pbk{bnk}", "convw", "convb"], w=[cn])
                    for tap, c0_ in ((1, 1), (0, 0)):
                        for (vg_, bnk, cbuf, cn) in vgl:
                            jj = j + vg_ * NJ
                            ps = pbank(bnk)
                            ph.add("dve", lambda e, ps=ps, cbuf=cbuf, jj=jj, tap=tap, c0_=c0_: e.scalar_tensor_tensor(
                                out=cbuf[:, :], in0=ps[:, c0_:c0_ + 256], scalar=convw[:, tap, jj:jj + 1], in1=cbuf[:, :],
                                op0=ALU.mult, op1=ALU.add), r=[f"pbk{bnk}", "convw", cn], w=[cn])
                    def tail(cvg=cvg, sgb=sgb, cvv=cvv, j=j, sub=sub, ng=ng, sgn=sgn, nv=nv):
                        ph.add("act", lambda e: e.activation(out=sgb[:, :], in_=cvg[:, :], func=AF.Silu), r=[ng], w=[sgn])
                        ph.add("pool", lambda e: e.tensor_tensor(out=gT[:, j, sub * 256:(sub + 1) * 256], in0=sgb[:, :], in1=cvv[:, :], op=ALU.mult),
                               r=[sgn, nv], w=[f"gT{j}"])
                    pipe5.defer(2, tail)
                    pipe5.tick()
            pipe5.flush()
            for it in range(8):
                i = tcb * 8 + it
                for hf in range(2):
                    bank = (it * 2 + hf) % 2
                    for j in range(NJ):
                        ph.add("pe", lambda e, j=j, it=it, hf=hf, bank=bank: e.matmul(pbank(bank), gT[:, j, it * 128:(it + 1) * 128], wd[:, j, hf * 512:(hf + 1) * 512],
                                                                                      start=(j == 0), stop=(j == NJ - 1)), r=[f"gT{j}", f"wd{j}"], w=[f"pbk{bank}"])
                    ph.add("dve", lambda e, i=i, hf=hf, bank=bank: e.tensor_tensor(out=h[:, i, hf * 512:(hf + 1) * 512], in0=pbank(bank), in1=h[:, i, hf * 512:(hf + 1) * 512], op=ALU.add),
                           r=[f"pbk{bank}", f"hf{i}"], w=[f"hf{i}"])
        if dbg:
            for i4 in range(4):
                ph.add("sp", lambda e, i4=i4, l=l: e.dma_start(out=dbg_d[l, i4 * 512:(i4 + 1) * 512, :].rearrange("(n p) d -> p n d", p=128), in_=h[:, i4 * 4:(i4 + 1) * 4, :]),
                       r=[f"hf{i}" for i in range(i4 * 4, i4 * 4 + 4)], w=["dbg"], dma=True)
        ph.emit()

    lb = Bump(nc, [(SB_X, SB_END)], "fin")
    g1 = lb.t("g1", [128, D], F32)
    gb = lb.t("gb", [128, D], F32)
    junk = lb.t("junk", [128, D], F32)
    ob = [lb.t(f"ob{i}", [128, D], F32) for i in range(2)]
    ph.add("sp", lambda e: e.dma_start(out=g1[0:1, :], in_=gfin_d[:, :]), w=["g1"], dma=True)
    for hf in range(2):
        ph.add("pe", lambda e, hf=hf: e.matmul(pbank(hf), ones[0:1, :], g1[0:1, hf * 512:(hf + 1) * 512], start=True, stop=True), r=["g1", "cf32"], w=[f"pbk{hf}"])
        ph.add("act", lambda e, hf=hf: e.activation(out=gb[:, hf * 512:(hf + 1) * 512], in_=pbank(hf), func=AF.Copy), r=[f"pbk{hf}"], w=["gb"])
    for i in range(NT):
        o = ob[i % 2]
        on = f"ob{i % 2}"
        ph.add("act", lambda e, i=i: e.activation(out=junk[:, :], in_=h[:, i, :], func=AF.Square, accum_out=ss[:, i:i + 1]), w=["junk", f"ss{i}"])
        ph.add("act", lambda e, i=i: e.activation(out=rstd[:, i:i + 1], in_=ss[:, i:i + 1], func=AF.Sqrt, scale=1.0 / D, bias=epsc[:, 0:1]),
               r=[f"ss{i}"], w=[f"rs{i}"])
        ph.add("dve", lambda e, i=i: e.reciprocal(out=rstd[:, i:i + 1], in_=rstd[:, i:i + 1]), r=[f"rs{i}"], w=[f"rs{i}"])
        ph.add("dve", lambda e, i=i, o=o: e.scalar_tensor_tensor(out=o[:, :], in0=h[:, i, :], scalar=rstd[:, i:i + 1], in1=gb[:, :], op0=ALU.mult, op1=ALU.mult),
               r=[f"rs{i}", "gb"], w=[on])
        ph.add("sp", lambda e, i=i, o=o: e.dma_start(out=out_d[i * 128:(i + 1) * 128, :], in_=o[:, :]), r=[on], w=["out"], dma=True)
    ph.emit()
    return nc


def _bf(a):
    return np.asarray(a, dtype=np.float32).astype(ml_dtypes.bfloat16)


def _prep(inputs, nlayers=4):
    f32 = np.float32
    x = np.asarray(inputs["x"], f32)
    mem = np.asarray(inputs["mem"], f32)
    w_in_dil = np.asarray(inputs["w_in_dil"], f32)
    perm = np.concatenate([np.arange(h * 64 + 32, h * 64 + 64).tolist() + np.arange(h * 64, h * 64 + 32).tolist() for h in range(12)]).astype(np.int64)
    qs = w_in_dil[:, :, 0:768][:, :, perm]
    ks = w_in_dil[:, :, 768:1536][:, :, perm]
    w_in_dil_x = np.ascontiguousarray(np.concatenate([w_in_dil, qs, ks], axis=2))
    gains = np.zeros((13, 1024), f32)
    for l in range(4):
        gains[3 * l + 0] = inputs["norm_mix"][l]
        gains[3 * l + 1] = inputs["norm_mem"][l]
        gains[3 * l + 2] = inputs["norm_ffn"][l]
    gains_l = np.ascontiguousarray(gains.reshape(13, 8, 128).transpose(2, 0, 1).reshape(128, 104))
    cw = np.asarray(inputs["conv_w"], f32)
    convw = np.ascontiguousarray(cw.reshape(4, 3, 44, 128).transpose(0, 3, 1, 2).reshape(4, 128, 132))
    convb = np.ascontiguousarray(np.asarray(inputs["conv_b"], f32).reshape(4, 44, 128).transpose(0, 2, 1))
    p = np.arange(128)[:, None]
    f = np.arange(128)[None, :]
    ident = (p == f).astype(f32)
    tri = (p <= f).astype(f32)
    atri = (p >= f).astype(f32)
    cbf = _bf(np.concatenate([ident, tri, atri], axis=1))
    U = (p <= f).astype(f32)
    ones = np.ones((128, 128), f32)
    sel = np.zeros((128, 128), f32)
    sel[127, :] = 1.0
    cf32 = np.ascontiguousarray(np.concatenate([U, ones, sel], axis=1))
    inv = (1.0 / (np.float32(10000.0) ** (np.arange(0, 64, 2, dtype=f32) / np.float32(64)))).astype(f32)
    common = {
        "w_in_fox": np.ascontiguousarray(inputs["w_in_fox"], f32), "w_in_dil": w_in_dil_x,
        "w_mem_kv": np.ascontiguousarray(inputs["w_mem_kv"], f32), "w_out": np.ascontiguousarray(inputs["w_out"], f32),
        "w_up": np.ascontiguousarray(inputs["w_up"], f32), "w_down": np.ascontiguousarray(inputs["w_down"], f32),
        "gains": gains_l, "gfin": np.ascontiguousarray(np.asarray(inputs["norm_final"], f32).reshape(1, 1024)),
        "convw": convw, "convb": convb, "b_forget": np.ascontiguousarray(inputs["b_forget"], f32),
        "cbf": cbf, "cf32": cf32,
    }
    maps = []
    for c in range(8):
        b, hf = c // 2, c % 2
        pos = (np.arange(T, dtype=f32) + np.float32(hf * T))
        ang = pos[:, None] * inv[None, :]
        cos = np.cos(ang).astype(f32).T
        sin = np.sin(ang).astype(f32).T
        cosF = np.concatenate([cos, cos, cos, cos], axis=0)
        sinF = np.concatenate([-sin, sin, -sin, sin], axis=0)
        flags = np.zeros((128, 2), f32)
        flags[:, 0] = NEG if hf == 0 else 0.0
        flags[:, 1] = 0.0 if hf == 0 else 1.0
        m = dict(common)
        m["x"] = np.ascontiguousarray(x[b, hf * T:(hf + 1) * T])
        m["mem"] = np.ascontiguousarray(mem[b])
        m["rope"] = np.ascontiguousarray(np.stack([cosF, sinF], axis=0))
        m["flags"] = flags
        maps.append(m)
    return maps


_NC_CACHE = {}


def kernel(**inputs):
    if "nc" not in _NC_CACHE:
        _NC_CACHE["nc"] = build(4)
    nc = _NC_CACHE["nc"]
    maps = _prep(inputs)
    res = run_bass_kernel_spmd(nc, maps, core_ids=list(range(8)))
    out = np.zeros((4, 2 * T, D), np.float32)
    for c in range(8):
        out[c // 2, (c % 2) * T:(c % 2 + 1) * T] = res.results[c]["out"]
    return out
```

```python
import numpy as np
import ml_dtypes
import concourse.bass as bass
import concourse.mybir as mybir
from concourse.bass_utils import run_bass_kernel_spmd

F32 = mybir.dt.float32
BF16 = mybir.dt.bfloat16
AF = mybir.ActivationFunctionType
ALU = mybir.AluOpType

T = 2048
NT = 16
D = 1024
KC = 8
DFF = 2816
NJ = 22
XW = 2064
VW = 194
NEG = -30000.0
PAIRS = [[0, 1], [2, 3], [4, 5], [6, 7]]

SB_H = 16512
SB_C = SB_H + 65536
SB_X = SB_C + 9216
SB_Q = SB_X + 33024
SB_HD = SB_Q + 32768
SB_L = SB_HD + 32768
SB_END = 229344


def sl(start, count, step):
    return slice(start, start + (count - 1) * step + 1, step)


class Ctx:
    pass


class Phase:
    def __init__(self, cx):
        self.cx = cx
        self.ops = []

    def add(self, eng, fn, r=(), w=(), dma=False):
        self.ops.append([eng, fn, tuple(r), tuple(w), dma])

    def emit(self):
        cx = self.cx
        nc = cx.nc
        ops = self.ops
        lastw = {}
        readers = {}
        deps_all = []
        needed = set()
        for i, (eng, fn, r, w, dma) in enumerate(ops):
            raw = set()
            wxx = set()
            for b in r:
                if b in lastw:
                    raw.add(lastw[b])
            for b in w:
                if b in lastw:
                    wxx.add(lastw[b])
                wxx |= readers.get(b, set())
            raw.discard(i)
            wxx.discard(i)
            fd = set()
            for d in raw:
                de, _, _, _, ddma = ops[d]
                if (not ddma) and (not dma) and de == eng and eng == "pe":
                    continue
                fd.add(d)
            for d in wxx:
                de, _, _, _, ddma = ops[d]
                if (not ddma) and (not dma) and de == eng:
                    continue
                fd.add(d)
            deps_all.append(fd)
            needed |= fd
            for b in r:
                readers.setdefault(b, set()).add(i)
            for b in w:
                lastw[b] = i
                readers[b] = set()
        sig = {}
        streams = {e: [] for e in ("pe", "act", "dve", "pool", "sp")}
        dma_used = {}
        for i, (eng, fn, r, w, dma) in enumerate(ops):
            wmax = {}
            for d in deps_all[i]:
                sm, vl = sig[d]
                if sm.name not in wmax or wmax[sm.name][1] < vl:
                    wmax[sm.name] = (sm, vl)
            waits = list(wmax.values())
            if dma:
                q = cx.dmaq[eng]
                slot = q["n"] % len(q["sems"])
                q["n"] += 1
                sem = q["sems"][slot]
                if q["cnt"][slot] > 0:
                    waits.append((sem, 16 * q["cnt"][slot]))
                q["cnt"][slot] += 1
                sig[i] = (sem, 16 * q["cnt"][slot])
                dma_used[(eng, slot)] = sig[i]
                streams[eng].append((waits, fn, sem, 16))
            else:
                if i in needed:
                    cx.cnt[eng] += 1
                    sig[i] = (cx.sem[eng], cx.cnt[eng])
                    streams[eng].append((waits, fn, cx.sem[eng], 1))
                else:
                    streams[eng].append((waits, fn, None, 0))
        finals = {e: [] for e in streams}
        for (eng, slot), s in dma_used.items():
            finals[eng].append(s)

        def mk(eng):
            lst = streams[eng]
            fin = finals[eng]
            waited = cx.waited[eng]

            def body(e):
                for waits, fn, sem, inc in lst:
                    for (s, v) in waits:
                        if waited.get(s.name, 0) < v:
                            e.wait_ge(s, v)
                            waited[s.name] = v
                    ins = fn(e)
                    if sem is not None:
                        ins.then_inc(sem, inc)
                for (s, v) in fin:
                    if waited.get(s.name, 0) < v:
                        e.wait_ge(s, v)
                        waited[s.name] = v
            return body

        with nc.Block() as blk:
            decos = {"pe": blk.tensor, "act": blk.scalar, "dve": blk.vector, "pool": blk.gpsimd, "sp": blk.sync}
            for eng in ("sp", "pool", "pe", "act", "dve"):
                if streams[eng] or finals[eng]:
                    decos[eng](mk(eng))
        self.ops = []


class Bump:
    def __init__(self, nc, regions, tag):
        self.nc = nc
        self.regions = [list(r) for r in regions]
        self.tag = tag
        self.k = 0

    def t(self, name, shape, dt):
        size = int(np.prod(shape[1:])) * (4 if dt == F32 else 2)
        size = (size + 31) // 32 * 32
        for rg in self.regions:
            if rg[0] + size <= rg[1]:
                off = rg[0]
                rg[0] += size
                self.k += 1
                return self.nc.alloc_sbuf_tensor_at(f"{self.tag}_{name}_{self.k}", list(shape), dt, offset=off)
        raise RuntimeError(f"SBUF bump overflow {self.tag} {name} {shape}")


def build(nlayers=4, dbg=False, stop=None):
    nc = bass.Bass("TRN2", target_bir_lowering=False)
    cx = Ctx()
    cx.nc = nc

    def din(name, shape, dt=F32):
        return nc.dram_tensor(name, list(shape), dt, kind="ExternalInput").ap()

    x_d = din("x", [T, D])
    mem_d = din("mem", [256, D])
    win_fox = din("w_in_fox", [2, D, 2572])
    win_dil = din("w_in_dil", [2, D, 4096])
    wmem_d = din("w_mem_kv", [4, D, 512])
    wout_d = din("w_out", [4, D, D])
    wup_d = din("w_up", [4, D, 2 * DFF])
    wdown_d = din("w_down", [4, DFF, D])
    gains_d = din("gains", [128, 13 * 8])
    gfin_d = din("gfin", [1, D])
    convw_d = din("convw", [4, 128, 3 * 44])
    convb_d = din("convb", [4, 128, 44])
    bfor_d = din("b_forget", [2, 12])
    cbf_d = din("cbf", [128, 384], BF16)
    cf32_d = din("cf32", [128, 384])
    rope_d = din("rope", [2, 128, T])
    flags_d = din("flags", [128, 2])
    out_d = nc.dram_tensor("out", [T, D], F32, kind="ExternalOutput").ap()
    dbg_d = None
    if dbg:
        dbg_d = nc.dram_tensor("dbg", [nlayers, T, D], F32, kind="ExternalOutput").ap()

    kmine = [nc.dram_tensor(f"kmine{g}", [384, T], BF16) for g in range(2)]
    kall = [nc.dram_tensor(f"kall{g}", [768, T], BF16) for g in range(2)]
    vmine = [nc.dram_tensor(f"vmine{g}", [T, 2 * VW], BF16) for g in range(3)]
    vall = [nc.dram_tensor(f"vall{g}", [2 * T, 2 * VW], BF16) for g in range(3)]
    emine = nc.dram_tensor("emine", [T, 12], F32)
    eall = nc.dram_tensor("eall", [2 * T, 12], F32)
    hmine = nc.dram_tensor("hmine", [128, 16], BF16)
    hall = nc.dram_tensor("hall", [256, 16], BF16)

    cx.sem = {}
    cx.cnt = {}
    cx.waited = {}
    cx.dmaq = {}
    for e in ("pe", "act", "dve", "pool", "sp"):
        cx.sem[e] = nc.alloc_semaphore(f"s_{e}")
        cx.cnt[e] = 0
        cx.waited[e] = {}
    for e, n in (("sp", 8), ("pool", 6), ("act", 4)):
        cx.dmaq[e] = {"sems": [nc.alloc_semaphore(f"d_{e}{i}") for i in range(n)], "cnt": [0] * n, "n": 0}
    ccsems = [nc.alloc_semaphore(f"cc{i}") for i in range(8 * nlayers)]

    h = nc.alloc_sbuf_tensor_at("h", [128, NT, D], F32, offset=SB_H)
    cb = Bump(nc, [(SB_C, SB_X)], "c")
    cbf = cb.t("cbf", [128, 384], BF16)
    ident = cbf[:, 0:128]
    tri = cbf[:, 128:256]
    dmask = cbf[:, 128:384]
    cf32 = cb.t("cf32", [128, 384], F32)
    Umat = cf32[:, 0:128]
    ones = cf32[:, 128:256]
    sel127 = cf32[:, 256:384]
    gains = cb.t("gains", [128, 13, 8], F32)
    convw = cb.t("convw", [128, 3, 44], F32)
    convb = cb.t("convb", [128, 44], F32)
    bfor = cb.t("bfor", [128, 12], F32)
    flags = cb.t("flags", [128, 2], F32)
    pmask = flags[:, 0:1]
    hflag = flags[:, 1:2]
    ss = cb.t("ss", [128, 32], F32)
    rstd = cb.t("rstd", [128, 32], F32)
    kmT = cb.t("kmT", [128, 2, 256], BF16)
    vaugm = cb.t("vaugm", [128, 2, 2, VW], BF16)
    dl = cb.t("dl", [128, NT, 12], F32)
    dmid = cb.t("dmid", [128, NT, 12], F32)
    epre = cb.t("epre", [128, NT, 12], F32)
    zcol = cb.t("zcol", [128, 2], F32)
    halo = cb.t("halo", [128, 8, 2], BF16)
    epsc = cb.t("epsc", [128, 2], F32)
    xnT = nc.alloc_sbuf_tensor_at("xnT", [128, KC, XW], BF16, offset=SB_X)
    qT = nc.alloc_sbuf_tensor_at("qT", [128, 8, T], BF16, offset=SB_Q)
    headsT = nc.alloc_sbuf_tensor_at("headsT", [128, 8, T], BF16, offset=SB_HD)
    mnT = nc.alloc_sbuf_tensor_at("mnT", [128, KC, 256], BF16, offset=SB_HD)

    pb = [nc.alloc_psum_tensor(f"pb{i}", [128, 512], F32) for i in range(8)]

    def pbank(i):
        return pb[i][:, :]

    def pcols(c0, w):
        assert c0 // 512 == (c0 + w - 1) // 512
        return pb[c0 // 512][:, c0 % 512: c0 % 512 + w]

    def pbank_bf(i):
        return pbank(i).bitcast(BF16)

    ph = Phase(cx)
    uid = [0]

    def U():
        uid[0] += 1
        return uid[0]

    ph.add("sp", lambda e: e.dma_start(out=cbf[:, :], in_=cbf_d[:, :]), w=["cbf"], dma=True)
    ph.add("sp", lambda e: e.dma_start(out=cf32[:, :], in_=cf32_d[:, :]), w=["cf32"], dma=True)
    ph.add("sp", lambda e: e.dma_start(out=gains[:, :, :], in_=gains_d.rearrange("p (a c) -> p a c", c=8)), w=["gains"], dma=True)
    ph.add("sp", lambda e: e.dma_start(out=flags[:, :], in_=flags_d[:, :]), w=["flags"], dma=True)
    for i4 in range(4):
        ph.add("sp", lambda e, i4=i4: e.dma_start(
            out=h[:, i4 * 4:(i4 + 1) * 4, :],
            in_=x_d[i4 * 512:(i4 + 1) * 512, :].rearrange("(n p) d -> p n d", p=128)), w=[f"h{i4}"], dma=True)
    ph.add("dve", lambda e: e.memset(zcol[:, :], 0.0), w=["zcol"])
    ph.add("dve", lambda e: e.memset(epsc[:, :], 1e-6), w=["epsc"])
    ph.add("dve", lambda e: e.memset(xnT[:, :, 0:2], 0.0), w=["xhalo"])
    if nlayers < 2:
        dum = cb.t("dum", [128, 4], F32)
        ph.add("sp", lambda e: e.dma_start(out=dum[:, 0:2], in_=rope_d[0][:, 0:2]), w=["dum0"], dma=True)
        ph.add("sp", lambda e: e.dma_start(out=dum[:, 2:4], in_=win_dil[0][0:128, 0:2]), w=["dum1"], dma=True)
    ph.emit()

    def rmsnorm_tile(ph, src, gidx, dst, loc, name, sidx):
        k = U()
        junk = loc["junk"]
        xh = loc["xh"][k % 2]
        xhn = f"xh{k % 2}"
        ph.add("act", lambda e: e.activation(out=junk[:, :], in_=src, func=AF.Square, accum_out=ss[:, sidx:sidx + 1]),
               r=[name], w=["junk", f"ss{sidx}"])
        ph.add("act", lambda e: e.activation(out=rstd[:, sidx:sidx + 1], in_=ss[:, sidx:sidx + 1], func=AF.Sqrt, scale=1.0 / D, bias=epsc[:, 0:1]),
               r=[f"ss{sidx}", "epsc"], w=[f"rs{sidx}"])
        ph.add("dve", lambda e: e.reciprocal(out=rstd[:, sidx:sidx + 1], in_=rstd[:, sidx:sidx + 1]), r=[f"rs{sidx}"], w=[f"rs{sidx}"])
        ph.add("dve", lambda e: e.tensor_scalar(out=xh[:, :], in0=src, scalar1=rstd[:, sidx:sidx + 1], scalar2=None,
                                                op0=ALU.mult), r=[name, f"rs{sidx}"], w=[xhn])
        bk = 4 + (k % 2)
        pt = pbank_bf(bk)
        for c in range(KC):
            ph.add("pe", lambda e, c=c: e.transpose(pt[:, c * 128:(c + 1) * 128], xh[:, c * 128:(c + 1) * 128], ident),
                   r=[xhn, "cbf"], w=[f"pbk{bk}"])
        g = gains[:, gidx, :].unsqueeze(2).to_broadcast([128, 8, 128])
        ph.add("dve", lambda e: e.tensor_tensor(out=dst, in0=pt[:, :].rearrange("p (c t) -> p c t", c=8), in1=g, op=ALU.mult),
               r=[f"pbk{bk}", "gains"], w=[f"nt_{name}"])

    for l in range(nlayers):
        fox = (l % 2 == 0)
        slot = l // 2
        win = win_fox[slot] if fox else win_dil[slot]

        lb = Bump(nc, [(SB_L, SB_END)], f"p1_{l}")
        loc = {"junk": lb.t("junk", [128, D], BF16), "xh": [lb.t("xh0", [128, D], BF16), lb.t("xh1", [128, D], BF16)]}
        memt = lb.t("memt", [128, 2, D], F32)
        ph.add("sp", lambda e: e.dma_start(out=memt[:, :, :], in_=mem_d.rearrange("(n p) d -> p n d", p=128)), w=["memt"], dma=True)
        ph.add("sp", lambda e, l=l: e.dma_start(out=convw[:, :, :], in_=convw_d[l].rearrange("p (a c) -> p a c", c=44)), w=["convw"], dma=True)
        ph.add("sp", lambda e, l=l: e.dma_start(out=convb[:, :], in_=convb_d[l]), w=["convb"], dma=True)
        if fox:
            ph.add("sp", lambda e, slot=slot: e.dma_start(out=bfor[:, :], in_=bfor_d[slot].partition_broadcast(128)), w=["bfor"], dma=True)
        for i in range(NT):
            rmsnorm_tile(ph, h[:, i, :], 3 * l + 0, xnT[:, :, 2 + i * 128: 2 + (i + 1) * 128], loc, f"h{i // 4}", i)
        for i in range(2):
            rmsnorm_tile(ph, memt[:, i, :], 3 * l + 1, mnT[:, :, i * 128:(i + 1) * 128], loc, "memt", 16 + i)
        ph.emit()
        if stop == "p1":
            break

        lb = Bump(nc, [(SB_HD + 4096, SB_L), (SB_L, SB_END)], f"p2_{l}")
        wq = [lb.t(f"wq{i}", [128, KC, 128], BF16) for i in range(4)]
        wv = [lb.t(f"wv{i}", [128, KC, 396], BF16) for i in range(2)]
        vst = [lb.t(f"vst{i}", [128, 6, VW], BF16) for i in range(2)]
        kst = [lb.t(f"kst{i}", [128, 512], BF16) for i in range(3)]
        wmv = lb.t("wmv", [128, KC, 256], BF16)
        if fox:
            flog = lb.t("flog", [128, NT, 12], F32)
            spb = lb.t("spb", [128, NT, 12], F32)
            totb = lb.t("totb", [128, NT, 12], F32)
            offb = lb.t("offb", [128, NT, 12], F32)
            exdb = lb.t("exdb", [128, NT, 12], F32)
        else:
            cosF = lb.t("cosF", [128, T], F32)
            sinF = lb.t("sinF", [128, T], F32)
            rt = [lb.t(f"rt{i}", [128, 512], F32) for i in range(4)]
            ph.add("sp", lambda e: e.dma_start(out=cosF[:, :], in_=rope_d[0]), w=["cosF"], dma=True)
            ph.add("sp", lambda e: e.dma_start(out=sinF[:, :], in_=rope_d[1]), w=["sinF"], dma=True)
        for i in range(2):
            ph.add("dve", lambda e, i=i: e.memset(vst[i][:, :, :], 0.0), w=[f"vst{i}"])
            ph.add("dve", lambda e, i=i: e.memset(vst[i][:, :, 64:65], 1.0), w=[f"vst{i}"])
            ph.add("dve", lambda e, i=i: e.memset(vst[i][:, :, 97:98], 1.0), w=[f"vst{i}"])

        wsrc = win.rearrange("(kc p) n -> p kc n", p=128)
        vcol0 = 1536
        vw = [384, 396 if fox else 384]
        for hf in range(2):
            ph.add("pool", lambda e, hf=hf: e.dma_start(out=wv[hf][:, :, 0:vw[hf]], in_=wsrc[:, :, vcol0 + hf * 384: vcol0 + hf * 384 + vw[hf]]),
                   w=[f"wv{hf}"], dma=True)

        qmcol = 2316 if fox else 2304
        chunks = []
        for c in range(6):
            chunks.append(("q", c, c * 128))
        for c in range(6):
            chunks.append(("k", c, 768 + c * 128))
        for c in range(2):
            chunks.append(("m", c, qmcol + c * 128))
        wk = [0]
        pk = [0]

        def load_w(col, src=None):
            i = wk[0] % 4
            wk[0] += 1
            s = wsrc if src is None else src
            ph.add("pool", lambda e: e.dma_start(out=wq[i][:, :, :], in_=s[:, :, col:col + 128]), w=[f"wq{i}"], dma=True)
            return i

        def proj_fm(wi, rhs_fn, n, bank):
            for kc in range(KC):
                ph.add("pe", lambda e, kc=kc: e.matmul(pbank(bank)[:, 0:n], wq[wi][:, kc, :], rhs_fn(kc), start=(kc == 0), stop=(kc == KC - 1)),
                       r=[f"wq{wi}", "xnT", "mnT"], w=[f"pbk{bank}"])

        kstk = [0]
        for (kind, c, col) in chunks:
            rope_on = (not fox) and kind in ("q", "k")
            wi = load_w(col)
            if rope_on:
                wis = load_w(2560 + (0 if kind == "q" else 768) + c * 128)
            for tc in range(4):
                bank = pk[0] % 2
                pk[0] += 1
                proj_fm(wi, lambda kc, tc=tc: xnT[:, kc, 2 + tc * 512: 2 + (tc + 1) * 512], 512, bank)
                if rope_on:
                    proj_fm(wis, lambda kc, tc=tc: xnT[:, kc, 2 + tc * 512: 2 + (tc + 1) * 512], 512, bank + 2)
                if kind == "k":
                    ks = kstk[0] % 3
                    kstk[0] += 1
                    dst = kst[ks][:, :]
                    dname = f"kst{ks}"
                else:
                    cc = c if kind == "q" else 6 + c
                    dst = qT[:, cc, tc * 512:(tc + 1) * 512]
                    dname = f"qT{cc}"
                if rope_on:
                    r0 = rt[(pk[0] % 2) * 2]
                    r1 = rt[(pk[0] % 2) * 2 + 1]
                    n0 = f"rt{(pk[0] % 2) * 2}"
                    n1 = f"rt{(pk[0] % 2) * 2 + 1}"
                    ph.add("dve", lambda e, bank=bank, tc=tc, r0=r0: e.tensor_tensor(out=r0[:, :], in0=pbank(bank), in1=cosF[:, tc * 512:(tc + 1) * 512], op=ALU.mult),
                           r=[f"pbk{bank}", "cosF"], w=[n0])
                    ph.add("dve", lambda e, bank=bank, tc=tc, r1=r1: e.tensor_tensor(out=r1[:, :], in0=pbank(bank + 2), in1=sinF[:, tc * 512:(tc + 1) * 512], op=ALU.mult),
                           r=[f"pbk{bank + 2}", "sinF"], w=[n1])
                    ph.add("dve", lambda e, r0=r0, r1=r1, dst=dst: e.tensor_tensor(out=dst, in0=r0[:, :], in1=r1[:, :], op=ALU.add),
                           r=[n0, n1], w=[dname])
                else:
                    ph.add("act", lambda e, bank=bank, dst=dst: e.activation(out=dst, in_=pbank(bank), func=AF.Copy), r=[f"pbk{bank}"], w=[dname])
                if kind == "k":
                    ph.add("sp", lambda e, c=c, tc=tc, dst=dst: e.dma_start(out=kmine[c // 3][(c % 3) * 128:(c % 3 + 1) * 128, tc * 512:(tc + 1) * 512], in_=dst),
                           r=[dname], w=["kmine"], dma=True)

        wmsrc = wmem_d[l].rearrange("(kc p) n -> p kc n", p=128)
        for c in range(2):
            wi = load_w(c * 128, src=wmsrc)
            bank = pk[0] % 2
            pk[0] += 1
            proj_fm(wi, lambda kc: mnT[:, kc, :], 256, bank)
            ph.add("act", lambda e, bank=bank, c=c: e.activation(out=kmT[:, c, :], in_=pbank(bank)[:, 0:256], func=AF.Copy), r=[f"pbk{bank}"], w=["kmT"])
        ph.add("pool", lambda e: e.dma_start(out=wmv[:, :, :], in_=wmsrc[:, :, 256:512]), w=["wmv"], dma=True)
        ph.add("dve", lambda e: e.memset(vaugm[:, :, :, :], 0.0), w=["vaugm"])
        ph.add("dve", lambda e: e.memset(vaugm[:, :, :, 64:65], 1.0), w=["vaugm"])
        ph.add("dve", lambda e: e.memset(vaugm[:, :, :, 97:98], 1.0), w=["vaugm"])
        for i in range(2):
            bank = pk[0] % 2
            pk[0] += 1
            for kc in range(KC):
                ph.add("pe", lambda e, kc=kc, i=i, bank=bank: e.matmul(pbank(bank)[:, 0:256], mnT[:, kc, i * 128:(i + 1) * 128], wmv[:, kc, :],
                                                                     start=(kc == 0), stop=(kc == KC - 1)), r=["mnT", "wmv"], w=[f"pbk{bank}"])
            pv = pbank(bank)[:, 0:256].rearrange("p (a b d) -> p a b d", a=2, b=2)
            ph.add("act", lambda e, i=i, pv=pv: e.activation(out=vaugm[:, i, :, 0:64], in_=pv[:, :, 0, :], func=AF.Copy), r=[f"pbk{bank}"], w=["vaugm"])
            ph.add("act", lambda e, i=i, pv=pv: e.activation(out=vaugm[:, i, :, 129:193], in_=pv[:, :, 1, :], func=AF.Copy), r=[f"pbk{bank}"], w=["vaugm"])

        for i in range(NT):
            vs = vst[i % 2]
            vsn = f"vst{i % 2}"
            for hf in range(2):
                bank = 2 + (pk[0] % 2)
                pk[0] += 1
                n = vw[hf]
                for kc in range(KC):
                    ph.add("pe", lambda e, kc=kc, i=i, hf=hf, bank=bank, n=n: e.matmul(pbank(bank)[:, 0:n], xnT[:, kc, 2 + i * 128: 2 + (i + 1) * 128], wv[hf][:, kc, 0:n],
                                                                                  start=(kc == 0), stop=(kc == KC - 1)), r=["xnT", f"wv{hf}"], w=[f"pbk{bank}"])
                pv = pbank(bank)[:, 0:384].rearrange("p (a b d) -> p a b d", a=3, b=2)
                ph.add("act", lambda e, vs=vs, hf=hf, pv=pv: e.activation(out=vs[:, hf * 3:(hf + 1) * 3, 0:64], in_=pv[:, :, 0, :], func=AF.Copy),
                       r=[f"pbk{bank}"], w=[vsn])
                ph.add("dve", lambda e, vs=vs, hf=hf, pv=pv: e.tensor_copy(out=vs[:, hf * 3:(hf + 1) * 3, 129:193], in_=pv[:, :, 1, :]),
                       r=[f"pbk{bank}"], w=[vsn])
                if fox and hf == 1:
                    ph.add("dve", lambda e, i=i, bank=bank: e.tensor_tensor(out=flog[:, i, :], in0=pbank(bank)[:, 384:396], in1=bfor[:, :], op=ALU.add),
                           r=[f"pbk{bank}", "bfor"], w=["flog"])
            for g in range(3):
                ph.add("sp", lambda e, i=i, vs=vs, g=g: e.dma_start(out=vmine[g][i * 128:(i + 1) * 128, :], in_=vs[:, 2 * g:2 * g + 2, :].rearrange("p a b -> p (a b)")),
                       r=[vsn], w=["vmine"], dma=True)

        if fox:
            fl2 = flog[:, :, :].rearrange("p a b -> p (a b)")
            sp2 = spb[:, :, :].rearrange("p a b -> p (a b)")
            ph.add("act", lambda e: e.activation(out=sp2, in_=fl2, func=AF.Exp, scale=-1.0), r=["flog"], w=["spb"])
            ph.add("act", lambda e: e.activation(out=sp2, in_=sp2, func=AF.Ln, bias=1.0), r=["spb"], w=["spb"])
            ph.add("pe", lambda e: e.matmul(pbank(0)[:, 0:192], Umat, sp2, start=True, stop=True), r=["spb", "cf32"], w=["pbk0"])
            ph.add("pe", lambda e: e.matmul(pbank(1)[:, 0:192], ones, sp2, start=True, stop=True), r=["spb", "cf32"], w=["pbk1"])
            ph.add("dve", lambda e: e.tensor_copy(out=totb[:, :, :].rearrange("p a b -> p (a b)"), in_=pbank(1)[:, 0:192]), r=["pbk1"], w=["totb"])
            ph.add("dve", lambda e: e.memset(offb[:, 0, :], 0.0), w=["offb"])
            for i in range(1, NT):
                ph.add("dve", lambda e, i=i: e.tensor_tensor(out=offb[:, i, :], in0=offb[:, i - 1, :], in1=totb[:, i - 1, :], op=ALU.add),
                       r=["offb", "totb"], w=["offb"])
            ph.add("dve", lambda e: e.tensor_tensor(out=dl[:, :, :].rearrange("p a b -> p (a b)"), in0=pbank(0)[:, 0:192],
                                                    in1=offb[:, :, :].rearrange("p a b -> p (a b)"), op=ALU.add), r=["pbk0", "offb"], w=["dl"])
            ph.add("pe", lambda e: e.matmul(pbank(2)[:, 0:192], sel127, dl[:, :, :].rearrange("p a b -> p (a b)"), start=True, stop=True),
                   r=["dl", "cf32"], w=["pbk2"])
            ph.add("dve", lambda e: e.tensor_copy(out=dmid[:, :, :].rearrange("p a b -> p (a b)"), in_=pbank(2)[:, 0:192]), r=["pbk2"], w=["dmid"])
            dtot = dmid[:, 15:16, :].to_broadcast([128, NT, 12])
            ph.add("dve", lambda e: e.tensor_tensor(out=exdb[:, :, :], in0=dl[:, :, :], in1=dtot, op=ALU.subtract), r=["dl", "dmid"], w=["exdb"])
            ph.add("sp", lambda e: e.dma_start(out=emine.ap().rearrange("(i p) h -> p i h", p=128), in_=exdb[:, :, :]), r=["exdb"], w=["emine"], dma=True)
        ph.emit()
        if stop == "p2":
            break

        cs = ccsems[8 * l: 8 * l + 8]

        def v2(t, a):
            return t.ap().rearrange("(a b) c -> a (b c)", a=a)

        def cc_body(e, cs=cs, fox=fox):
            for g in range(2):
                e.collective_compute("AllGather", ALU.bypass, replica_groups=PAIRS, ins=[v2(kmine[g], 128)], outs=[v2(kall[g], 256)]).then_inc(cs[g])
            for g in range(3):
                e.collective_compute("AllGather", ALU.bypass, replica_groups=PAIRS, ins=[v2(vmine[g], 128)], outs=[v2(vall[g], 256)]).then_inc(cs[2 + g])
            if fox:
                e.collective_compute("AllGather", ALU.bypass, replica_groups=PAIRS, ins=[v2(emine, 128)], outs=[v2(eall, 256)]).then_inc(cs[5])
            for g in range(6 if fox else 5):
                e.wait_ge(cs[g], 1)
        with nc.Block() as blk:
            blk.gpsimd(cc_body)
        if stop == "cc1":
            break

        lbx = Bump(nc, [(SB_X, SB_Q)], f"p3x_{l}")
        lbl = Bump(nc, [(SB_L, SB_END)], f"p3l_{l}")
        kTb = [lbx.t(f"kT{i}", [128, 2 * T], BF16) for i in range(2)]
        nfb = 2 if fox else 1
        bcs = [(lbx if fox else lbl).t(f"bcs{i}", [128, 512], F32) for i in range(nfb)]
        rden = [(lbx if fox else lbl).t(f"rden{i}", [128, 512], F32) for i in range(nfb)]
        fin_k = [0]

        class Pipe:
            def __init__(self):
                self.q = []

            def defer(self, n, fn):
                self.q.append([n, fn])

            def tick(self):
                for it in self.q:
                    it[0] -= 1
                ready = [it for it in self.q if it[0] <= 0]
                self.q = [it for it in self.q if it[0] > 0]
                for it in ready:
                    it[1]()

            def flush(self):
                while self.q:
                    self.tick()

        pipe = Pipe()
        SK = 3 if fox else 2
        SK_DIL = 2
        DIL_POOL_ONLY = True

        def finalize(acc_ap, accname, odd, dst, dname, sbuf_acc=False, delay=4):
            k = fin_k[0] % nfb
            fb = 6 + (fin_k[0] % 2) if fox else 7
            fin_k[0] += 1
            dp = 32 if odd else 64
            r0 = 64 if odd else 0
            if sbuf_acc:
                rd = acc_ap[dp:dp + 1, :]
                rdn = accname
            else:
                rd = rden[k][dp:dp + 1, :]
                rdn = f"rden{k}"
            ph.add("dve", lambda e: e.reciprocal(out=rd, in_=acc_ap[dp:dp + 1, :]), r=[accname], w=[rdn])

            def stage_b():
                ph.add("pe", lambda e: e.matmul(pbank(fb), ones[dp:dp + 1, :], rd, start=True, stop=True), r=[rdn, "cf32"], w=[f"pbk{fb}"])
                if fox:
                    ph.add("dve", lambda e: e.tensor_copy(out=bcs[k][r0:r0 + 64, :], in_=pbank(fb)[r0:r0 + 64, :]), r=[f"pbk{fb}"], w=[f"bcs{k}"])
                else:
                    ph.add("act", lambda e: e.activation(out=bcs[k][r0:r0 + 64, :], in_=pbank(fb)[r0:r0 + 64, :], func=AF.Copy), r=[f"pbk{fb}"], w=[f"bcs{k}"])
                ph.add("dve", lambda e: e.tensor_tensor(out=dst, in0=acc_ap[r0:r0 + 64, :], in1=bcs[k][r0:r0 + 64, :], op=ALU.mult),
                       r=[accname, f"bcs{k}"], w=[dname])
            stage_b()

        def vcols(odd):
            return (65, 193) if odd else (0, 65)

        acck = [0]
        sk = [0]
        ptk = [0]

        if fox:
            vgb = [lbl.t(f"vg{i}", [128, 32, VW], BF16) for i in range(2)]
            pT = [lbl.t(f"pT{i}", [128, 512], BF16) for i in range(6)]
            bt = [lbx.t(f"bt{i}", [128, 8, 32], F32) for i in range(4)]
            ph.add("sp", lambda e: e.dma_start(out=epre[:, :, :], in_=eall.ap()[0:T, :].rearrange("(i p) h -> p i h", p=128)), w=["epre"], dma=True)
            ph.add("dve", lambda e: e.tensor_scalar(out=epre[:, :, :], in0=epre[:, :, :], scalar1=pmask, scalar2=None, op0=ALU.add), r=["epre", "flags"], w=["epre"])
            btk = [0]
            for j in range(6):
                kb = kTb[j % 2]
                kn = f"kT{j % 2}"
                vg = vgb[j % 2]
                vn = f"vg{j % 2}"
                ph.add("sp", lambda e, j=j, kb=kb: e.dma_start(out=kb[:, 0:T], in_=kall[j // 3].ap()[(j % 3) * 128:(j % 3 + 1) * 128, :]), w=[kn], dma=True)
                ph.add("sp", lambda e, j=j, kb=kb: e.dma_start(out=kb[:, T:2 * T], in_=kmine[j // 3].ap()[(j % 3) * 128:(j % 3 + 1) * 128, :]), w=[kn], dma=True)
                for q4 in range(4):
                    ph.add("sp", lambda e, j=j, vg=vg, q4=q4: e.dma_start(out=vg[:, q4 * 4:(q4 + 1) * 4, :],
                           in_=vall[j // 2].ap()[q4 * 512:(q4 + 1) * 512, (j % 2) * VW:(j % 2 + 1) * VW].rearrange("(i p) w -> p i w", p=128)), w=[vn], dma=True)
                    ph.add("sp", lambda e, j=j, vg=vg, q4=q4: e.dma_start(out=vg[:, 16 + q4 * 4:16 + (q4 + 1) * 4, :],
                           in_=vmine[j // 2].ap()[q4 * 512:(q4 + 1) * 512, (j % 2) * VW:(j % 2 + 1) * VW].rearrange("(i p) w -> p i w", p=128)), w=[vn], dma=True)
                for hh in range(2):
                    hd = 2 * j + hh
                    odd = (hh == 1)
                    r0 = 64 * hh
                    bi = btk[0] % 4
                    btk[0] += 1
                    btt = bt[bi]
                    btn = f"bt{bi}"
                    for m in range(8):
                        ph.add("dve", lambda e, m=m, hd=hd, btt=btt: e.tensor_scalar(out=btt[:, m, 0:16], in0=epre[:, :, hd], scalar1=dmid[:, 2 * m, hd:hd + 1],
                                                                                   scalar2=None, op0=ALU.subtract), r=["epre", "dmid"], w=[btn + f"a{m}"])
                        ph.add("dve", lambda e, m=m, hd=hd, btt=btt: e.tensor_scalar(out=btt[:, m, 16:32], in0=dl[:, :, hd], scalar1=dmid[:, 2 * m, hd:hd + 1],
                                                                                   scalar2=None, op0=ALU.subtract), r=["dl", "dmid"], w=[btn + f"b{m}"])
                    v0, v1 = vcols(odd)
                    M = 128 if odd else 65
                    for qc in range(4):
                        ab = acck[0] % 2
                        acck[0] += 1
                        accn = f"pbk{ab}"
                        diag = [16 + 4 * qc + kj for kj in range(4)]
                        full = list(range(1, 16)) + list(range(16, 16 + 4 * qc))
                        order = [0] + diag + full
                        for idx, kt in enumerate(order):
                            kj = kt - (16 + 4 * qc) if kt in diag else 0
                            c0 = kj * 128
                            sb = 2 + (sk[0] % 4)
                            sk[0] += 1
                            ph.add("pe", lambda e, kb=kb, kt=kt, c0=c0, sb=sb, qc=qc, j=j, r0=r0: e.matmul(
                                pbank(sb)[:, c0:512], kb[r0:r0 + 64, kt * 128:(kt + 1) * 128], qT[r0:r0 + 64, j, qc * 512 + c0:(qc + 1) * 512],
                                start=True, stop=True), r=[kn, f"qT{j}"], w=[f"pbk{sb}"])
                            pi = ptk[0] % 6
                            ptk[0] += 1
                            pt_ = pT[pi]
                            ptn = f"pT{pi}"
                            rn = []
                            for hf in range(2):
                                a0 = max(c0, hf * 256)
                                a1 = (hf + 1) * 256
                                if a0 >= a1:
                                    continue
                                m = 2 * qc + hf
                                src_b = btn + (f"a{m}" if kt < 16 else f"b{m}")
                                ph.add("act", lambda e, sb=sb, a0=a0, a1=a1, m=m, kt=kt, pt_=pt_, btt=btt: e.activation(
                                    out=pt_[:, a0:a1], in_=pbank(sb)[:, a0:a1], func=AF.Exp, scale=0.125, bias=btt[:, m, kt:kt + 1]),
                                    r=[f"pbk{sb}", src_b], w=[ptn + f"h{hf}"])
                                rn.append(ptn + f"h{hf}")
                            if kt in diag:
                                hfd = c0 // 256
                                ph.add("pool", lambda e, pt_=pt_, c0=c0: e.tensor_tensor(out=pt_[:, c0:c0 + 128], in0=pt_[:, c0:c0 + 128], in1=tri, op=ALU.mult),
                                       r=[ptn + f"h{hfd}", "cbf"], w=[ptn + f"h{hfd}"])
                            last = (idx == len(order) - 1)

                            def pv(vg=vg, kt=kt, v0=v0, v1=v1, pt_=pt_, c0=c0, ab=ab, idx=idx, M=M, last=last, rn=tuple(rn), vn=vn, accn=accn,
                                   odd=odd, r0=r0, j=j, qc=qc):
                                ph.add("pe", lambda e: e.matmul(pbank(ab)[0:M, c0:512], vg[:, kt, v0:v1], pt_[:, c0:512], start=(idx == 0), stop=last),
                                       r=[vn] + list(rn), w=[accn])
                                if last:
                                    finalize(pbank(ab), accn, odd, headsT[r0:r0 + 64, j, qc * 512:(qc + 1) * 512], f"hT{j}")
                            pipe.defer(SK, pv)
                            pipe.tick()
        else:
            pipe.flush()
            accs = [lbx.t(f"accs{i}", [128, T], F32) for i in range(2)]
            pT = [lbl.t(f"pT{i}", [128, 256], BF16) for i in range(8)]
            BR = [(1, 17), (4, 20), (16, 32)]
            ntl = 17 + 20 + 32
            vgd = lbl.t("vgd", [128, ntl, VW], BF16)
            mk_eng = [0]
            for j in range(6):
                kb = kTb[j % 2]
                kn = f"kT{j % 2}"
                ph.add("sp", lambda e, j=j, kb=kb: e.dma_start(out=kb[:, 0:T], in_=kall[j // 3].ap()[(j % 3) * 128:(j % 3 + 1) * 128, :]), w=[kn], dma=True)
                ph.add("sp", lambda e, j=j, kb=kb: e.dma_start(out=kb[:, T:2 * T], in_=kmine[j // 3].ap()[(j % 3) * 128:(j % 3 + 1) * 128, :]), w=[kn], dma=True)
                tbase = {}
                ti = 0
                for (d, _) in BR:
                    nbo = 16 // d
                    for r in range(d):
                        tbase[(d, r)] = ti
                        p0 = (nbo - 1) * 128 * d + r
                        src = vall[j // 2].ap()[sl(p0, 128, d), (j % 2) * VW:(j % 2 + 1) * VW] if d > 1 else vall[j // 2].ap()[p0:p0 + 128, (j % 2) * VW:(j % 2 + 1) * VW]
                        ph.add("sp", lambda e, src=src, ti=ti: e.dma_start(out=vgd[:, ti, :], in_=src), w=[f"vgd{d}"], dma=True)
                        if d > 1:
                            src2 = vmine[j // 2].ap()[r: T: d, (j % 2) * VW:(j % 2 + 1) * VW].rearrange("(n p) w -> p n w", p=128)
                        else:
                            src2 = vmine[j // 2].ap()[:, (j % 2) * VW:(j % 2 + 1) * VW].rearrange("(n p) w -> p n w", p=128)
                        if nbo >= 8:
                            for s4 in range(nbo // 4):
                                ph.add("sp", lambda e, src2=src2, ti=ti, s4=s4: e.dma_start(out=vgd[:, ti + 1 + s4 * 4: ti + 1 + (s4 + 1) * 4, :], in_=src2[:, s4 * 4:(s4 + 1) * 4, :]),
                                       w=[f"vgd{d}"], dma=True)
                        else:
                            ph.add("sp", lambda e, src2=src2, ti=ti, nbo=nbo: e.dma_start(out=vgd[:, ti + 1: ti + 1 + nbo, :], in_=src2), w=[f"vgd{d}"], dma=True)
                        ti += 1 + nbo
                for hh in range(2):
                    odd = (hh == 1)
                    r0 = 64 * hh
                    v0, v1 = vcols(odd)
                    M = 128 if odd else 65
                    acs = accs[hh]
                    acn = f"accs{hh}"
                    for bi_, (d, _) in enumerate(BR):
                        nbo = 16 // d
                        units = [(r, kk) for r in range(d) for kk in range(nbo + 1)]
                        for ui, (r, kk) in enumerate(units):
                            tb = tbase[(d, r)]
                            if kk == 0:
                                kpos0 = (nbo - 1) * 128 * d + r
                            else:
                                kpos0 = T + (kk - 1) * 128 * d + r
                            qb0 = max(kk - 1, 0)
                            qb1 = min(kk, nbo - 1)
                            nq = (qb1 - qb0 + 1) * 128
                            qpos0 = qb0 * 128 * d + r
                            sb = 4 + (sk[0] % 3)
                            sk[0] += 1
                            ph.add("pe", lambda e, kb=kb, kpos0=kpos0, d=d, sb=sb, qpos0=qpos0, nq=nq, j=j, r0=r0: e.matmul(
                                pbank(sb)[:, 0:nq], kb[r0:r0 + 64, sl(kpos0, 128, d)], qT[r0:r0 + 64, j, sl(qpos0, nq, d)],
                                start=True, stop=True), r=[kn, f"qT{j}"], w=[f"pbk{sb}"])
                            pi = ptk[0] % 8
                            ptk[0] += 1
                            pt_ = pT[pi]
                            ptn = f"pT{pi}"
                            bias_ap = pmask if kk == 0 else zcol[:, 0:1]
                            ph.add("act", lambda e, sb=sb, nq=nq, pt_=pt_, bias_ap=bias_ap: e.activation(
                                out=pt_[:, 0:nq], in_=pbank(sb)[:, 0:nq], func=AF.Exp, scale=0.125, bias=bias_ap), r=[f"pbk{sb}", "flags", "zcol"], w=[ptn])
                            if kk == 0:
                                msk = dmask[:, 128:256]
                            elif kk == nbo:
                                msk = dmask[:, 0:128]
                            else:
                                msk = dmask[:, 0:256]
                            meng = "pool" if (DIL_POOL_ONLY or mk_eng[0] % 3 == 0) else "dve"
                            mk_eng[0] += 1
                            ph.add(meng, lambda e, pt_=pt_, nq=nq, msk=msk: e.tensor_tensor(out=pt_[:, 0:nq], in0=pt_[:, 0:nq], in1=msk, op=ALU.mult),
                                   r=[ptn, "cbf"], w=[ptn])
                            lastu = (ui == len(units) - 1)

                            def pv(kk=kk, tb=tb, v0=v0, v1=v1, pt_=pt_, ptn=ptn, qb0=qb0, qb1=qb1, r=r, nbo=nbo, M=M, lastu=lastu, d=d, bi_=bi_,
                                   acs=acs, acn=acn, odd=odd, r0=r0, j=j):
                                for qi, qb in enumerate(range(qb0, qb1 + 1)):
                                    same = (qb == kk - 1)
                                    col = (r * nbo + qb) * 128
                                    bnk = col // 512
                                    ph.add("pe", lambda e, qi=qi, col=col, same=same: e.matmul(
                                        pcols(col, 128)[0:M, :], vgd[:, tb + kk, v0:v1], pt_[:, qi * 128:(qi + 1) * 128], start=(not same), stop=same),
                                        r=[f"vgd{d}", ptn], w=[f"pbk{bnk}"])
                                if lastu:
                                    wdt = T // d
                                    for rr in range(d):
                                        for s in range(max(1, wdt // 512)):
                                            w_ = min(wdt, 512)
                                            c_src = rr * wdt + s * w_
                                            src = pcols(c_src, w_)
                                            if d == 1:
                                                dsta = acs[:, s * 512:(s + 1) * 512]
                                            else:
                                                dsta = acs[:, sl(rr + s * w_ * d, w_, d)]
                                            bnk = c_src // 512
                                            if bi_ == 0:
                                                ph.add("act", lambda e, src=src, dsta=dsta: e.activation(out=dsta, in_=src, func=AF.Copy), r=[f"pbk{bnk}"], w=[acn + f"s{s}"])
                                            else:
                                                ph.add("dve", lambda e, src=src, dsta=dsta: e.tensor_tensor(out=dsta, in0=src, in1=dsta, op=ALU.add),
                                                       r=[f"pbk{bnk}"] + [acn + f"s{x}" for x in range(4)], w=[acn + f"s{x}" for x in range(4)])
                                    if bi_ == 2:
                                        for qc in range(4):
                                            finalize(acs[:, qc * 512:(qc + 1) * 512], acn + f"s{qc}", odd, headsT[r0:r0 + 64, j, qc * 512:(qc + 1) * 512], f"hT{j}", sbuf_acc=True)
                            pipe.defer(SK_DIL, pv)
                            pipe.tick()
                pipe.flush()

        if True:
            pTm = [lbl.t(f"pTm{i}", [128, 512], BF16) for i in range(4)]
            mk = [0]
            nsb = 4 if fox else 3
            sb0 = 2 if fox else 4
            for j in range(2):
                for hh in range(2):
                    odd = (hh == 1)
                    r0 = 64 * hh
                    v0, v1 = vcols(odd)
                    M = 128 if odd else 65
                    for qc in range(4):
                        ab = acck[0] % 2
                        acck[0] += 1
                        for kt in range(2):
                            sb = sb0 + (sk[0] % nsb)
                            sk[0] += 1
                            pi = mk[0] % 4
                            mk[0] += 1
                            ph.add("pe", lambda e, j=j, kt=kt, sb=sb, qc=qc, r0=r0: e.matmul(
                                pbank(sb), kmT[r0:r0 + 64, j, kt * 128:(kt + 1) * 128], qT[r0:r0 + 64, 6 + j, qc * 512:(qc + 1) * 512], start=True, stop=True),
                                r=["kmT", f"qT{6 + j}"], w=[f"pbk{sb}"])
                            ph.add("act", lambda e, sb=sb, pi=pi: e.activation(out=pTm[pi][:, :], in_=pbank(sb), func=AF.Exp, scale=0.125, bias=zcol[:, 0:1]),
                                   r=[f"pbk{sb}", "zcol"], w=[f"pTm{pi}"])

                            def pvm(j=j, kt=kt, v0=v0, v1=v1, pi=pi, ab=ab, M=M, odd=odd, r0=r0, qc=qc):
                                ph.add("pe", lambda e: e.matmul(pbank(ab)[0:M, :], vaugm[:, kt, j, v0:v1], pTm[pi][:, :], start=(kt == 0), stop=(kt == 1)),
                                       r=["vaugm", f"pTm{pi}"], w=[f"pbk{ab}"])
                                if kt == 1:
                                    finalize(pbank(ab), f"pbk{ab}", odd, headsT[r0:r0 + 64, 6 + j, qc * 512:(qc + 1) * 512], f"hT{6 + j}", delay=2)
                            pipe.defer(SK, pvm)
                            pipe.tick()
        pipe.flush()
        ph.emit()
        if stop == "p3":
            break

        lb = Bump(nc, [(SB_Q, SB_HD), (SB_L, SB_END)], f"p4_{l}")
        wo = lb.t("wo", [128, KC, D], BF16)
        loc = {"junk": lb.t("junk", [128, D], BF16), "xh": [lb.t("xh0", [128, D], BF16), lb.t("xh1", [128, D], BF16)]}
        wosrc = wout_d[l].rearrange("(kc p) n -> p kc n", p=128)
        for kc in range(KC):
            ph.add("pool", lambda e, kc=kc: e.dma_start(out=wo[:, kc, :], in_=wosrc[:, kc, :]), w=[f"wo{kc}"], dma=True)
        ok = [0]

        def outproj_tile(i):
            for hf in range(2):
                bank = ok[0] % 2
                ok[0] += 1
                for kc in range(KC):
                    ph.add("pe", lambda e, kc=kc, i=i, hf=hf, bank=bank: e.matmul(pbank(bank), headsT[:, kc, i * 128:(i + 1) * 128], wo[:, kc, hf * 512:(hf + 1) * 512],
                                                                                  start=(kc == 0), stop=(kc == KC - 1)), r=[f"hT{kc}", f"wo{kc}"], w=[f"pbk{bank}"])
                ph.add("dve", lambda e, i=i, hf=hf, bank=bank: e.tensor_tensor(out=h[:, i, hf * 512:(hf + 1) * 512], in0=pbank(bank), in1=h[:, i, hf * 512:(hf + 1) * 512], op=ALU.add),
                       r=[f"pbk{bank}", f"ht{i}"], w=[f"ht{i}"])

        def norm2_tile(i):
            rmsnorm_tile(ph, h[:, i, :], 3 * l + 2, xnT[:, :, 2 + i * 128: 2 + (i + 1) * 128], loc, f"ht{i}", i)

        outproj_tile(NT - 1)
        norm2_tile(NT - 1)
        ph.add("sp", lambda e: e.dma_start(out=hmine.ap().rearrange("p (c t) -> p c t", t=2), in_=xnT[:, :, 2 + T - 2: 2 + T]), r=[f"nt_ht{NT - 1}"], w=["hmine"], dma=True)
        ph.emit()
        cs3 = ccsems[8 * l + 6]

        def cc2_body(e, cs3=cs3):
            e.collective_compute("AllGather", ALU.bypass, replica_groups=PAIRS, ins=[hmine.ap().opt()], outs=[hall.ap().opt()]).then_inc(cs3)
            e.wait_ge(cs3, 1)
        with nc.Block() as blk:
            blk.gpsimd(cc2_body)
        ph.add("sp", lambda e: e.dma_start(out=halo[:, :, :], in_=hall.ap()[0:128, :].rearrange("p (c t) -> p c t", t=2)), w=["halo"], dma=True)
        ph.add("dve", lambda e: e.tensor_scalar(out=xnT[:, :, 0:2], in0=halo[:, :, :], scalar1=hflag, scalar2=None, op0=ALU.mult), r=["halo", "flags"], w=["xhalo"])
        for i in range(NT - 1):
            outproj_tile(i)
            if i >= 1:
                norm2_tile(i - 1)
        norm2_tile(NT - 2)
        ph.emit()

        if stop == "p4":
            break
        wd = nc.alloc_sbuf_tensor_at(f"wd_{l}", [128, NJ, D], BF16, offset=SB_Q)
        gT = nc.alloc_sbuf_tensor_at(f"gT_{l}", [128, NJ, 1024], BF16, offset=SB_Q + 45056)
        lb = Bump(nc, [(SB_Q + 2 * 45056, SB_END)], f"p5_{l}")
        wu = [lb.t(f"wu{i}", [128, KC, 2, 128], BF16) for i in range(2)]
        cv = [lb.t(f"cv{i}", [128, 256], F32) for i in range(4)]
        sg = [lb.t(f"sg{i}", [128, 256], F32) for i in range(2)]
        wdsrc = wdown_d[l].rearrange("(j p) n -> p j n", p=128)
        pipe5 = Pipe()
        wusrc = wup_d[l].rearrange("(kc p) n -> p kc n", p=128)
        for tcb in range(2):
            for j in range(NJ):
                if tcb == 0 and j % 2 == 0:
                    ph.add("pool", lambda e, j=j: e.dma_start(out=wd[:, j:j + 2, :], in_=wdsrc[:, j:j + 2, :]), w=[f"wd{j}", f"wd{j + 1}"], dma=True)
                wi = (tcb * NJ + j) % 2
                ph.add("pool", lambda e, j=j, wi=wi: e.dma_start(out=wu[wi][:, :, 0, :], in_=wusrc[:, :, j * 128:(j + 1) * 128]), w=[f"wu{wi}"], dma=True)
                ph.add("pool", lambda e, j=j, wi=wi: e.dma_start(out=wu[wi][:, :, 1, :], in_=wusrc[:, :, DFF + j * 128: DFF + (j + 1) * 128]), w=[f"wu{wi}"], dma=True)
                for sub in range(4):
                    k = U()
                    tok0 = tcb * 1024 + sub * 256
                    pv_b = 2 + (k % 3) * 2
                    pg_b = pv_b + 1
                    for (vg_, bnk) in ((0, pv_b), (1, pg_b)):
                        for kc in range(KC):
                            ph.add("pe", lambda e, kc=kc, vg_=vg_, bnk=bnk, wi=wi, tok0=tok0: e.matmul(pbank(bnk)[:, 0:258], wu[wi][:, kc, vg_, :], xnT[:, kc, tok0:tok0 + 258],
                                                                                             start=(kc == 0), stop=(kc == KC - 1)), r=[f"wu{wi}", "xnT", "xhalo"], w=[f"pbk{bnk}"])
                    cvv = cv[(k % 2) * 2]
                    cvg = cv[(k % 2) * 2 + 1]
                    nv = f"cv{(k % 2) * 2}"
                    ng = f"cv{(k % 2) * 2 + 1}"
                    sgb = sg[k % 2]
                    sgn = f"sg{k % 2}"
                    vgl = ((0, pv_b, cvv, nv), (1, pg_b, cvg, ng))
                    for (vg_, bnk, cbuf, cn) in vgl:
                        jj = j + vg_ * NJ
                        ps = pbank(bnk)
                        ph.add("act", lambda e, ps=ps, cbuf=cbuf, jj=jj: e.activation(out=cbuf[:, :], in_=ps[:, 2:258], func=AF.Identity,
                                                                                      scale=convw[:, 2, jj:jj + 1], bias=convb[:, jj:jj + 1]),
                               r=[f"pbk{bnk}", "convw", "convb"], w=[cn])
                    for tap, c0_ in ((1, 1), (0, 0)):
                        for (vg_, bnk, cbuf, cn) in vgl:
                            jj = j + vg_ * NJ
                            ps = pbank(bnk)
                            ph.add("dve", lambda e, ps=ps, cbuf=cbuf, jj=jj, tap=tap, c0_=c0_: e.scalar_tensor_tensor(
                                out=cbuf[:, :], in0=ps[:, c0_:c0_ + 256], scalar=convw[:, tap, jj:jj + 1], in1=cbuf[:, :],
                                op0=ALU.mult, op1=ALU.add), r=[f"pbk{bnk}", "convw", cn], w=[cn])
                    def tail(cvg=cvg, sgb=sgb, cvv=cvv, j=j, sub=sub, ng=ng, sgn=sgn, nv=nv):
                        ph.add("act", lambda e: e.activation(out=sgb[:, :], in_=cvg[:, :], func=AF.Silu), r=[ng], w=[sgn])
                        ph.add("pool", lambda e: e.tensor_tensor(out=gT[:, j, sub * 256:(sub + 1) * 256], in0=sgb[:, :], in1=cvv[:, :], op=ALU.mult),
                               r=[sgn, nv], w=[f"gT{j}"])
                    pipe5.defer(2, tail)
                    pipe5.tick()
            pipe5.flush()
            for it in range(8):
                i = tcb * 8 + it
                for hf in range(2):
                    bank = (it * 2 + hf) % 2
                    for j in range(NJ):
                        ph.add("pe", lambda e, j=j, it=it, hf=hf, bank=bank: e.matmul(pbank(bank), gT[:, j, it * 128:(it + 1) * 128], wd[:, j, hf * 512:(hf + 1) * 512],
                                                                                      start=(j == 0), stop=(j == NJ - 1)), r=[f"gT{j}", f"wd{j}"], w=[f"pbk{bank}"])
                    ph.add("dve", lambda e, i=i, hf=hf, bank=bank: e.tensor_tensor(out=h[:, i, hf * 512:(hf + 1) * 512], in0=pbank(bank), in1=h[:, i, hf * 512:(hf + 1) * 512], op=ALU.add),
                           r=[f"pbk{bank}", f"hf{i}"], w=[f"hf{i}"])
        if dbg:
            for i4 in range(4):
                ph.add("sp", lambda e, i4=i4, l=l: e.dma_start(out=dbg_d[l, i4 * 512:(i4 + 1) * 512, :].rearrange("(n p) d -> p n d", p=128), in_=h[:, i4 * 4:(i4 + 1) * 4, :]),
                       r=[f"hf{i}" for i in range(i4 * 4, i4 * 4 + 4)], w=["dbg"], dma=True)
        ph.emit()

    lb = Bump(nc, [(SB_X, SB_END)], "fin")
    g1 = lb.t("g1", [128, D], F32)
    gb = lb.t("gb", [128, D], F32)
    junk = lb.t("junk", [128, D], F32)
    ob = [lb.t(f"ob{i}", [128, D], F32) for i in range(2)]
    ph.add("sp", lambda e: e.dma_start(out=g1[0:1, :], in_=gfin_d[:, :]), w=["g1"], dma=True)
    for hf in range(2):
        ph.add("pe", lambda e, hf=hf: e.matmul(pbank(hf), ones[0:1, :], g1[0:1, hf * 512:(hf + 1) * 512], start=True, stop=True), r=["g1", "cf32"], w=[f"pbk{hf}"])
        ph.add("act", lambda e, hf=hf: e.activation(out=gb[:, hf * 512:(hf + 1) * 512], in_=pbank(hf), func=AF.Copy), r=[f"pbk{hf}"], w=["gb"])
    for i in range(NT):
        o = ob[i % 2]
        on = f"ob{i % 2}"
        ph.add("act", lambda e, i=i: e.activation(out=junk[:, :], in_=h[:, i, :], func=AF.Square, accum_out=ss[:, i:i + 1]), w=["junk", f"ss{i}"])
        ph.add("act", lambda e, i=i: e.activation(out=rstd[:, i:i + 1], in_=ss[:, i:i + 1], func=AF.Sqrt, scale=1.0 / D, bias=epsc[:, 0:1]),
               r=[f"ss{i}"], w=[f"rs{i}"])
        ph.add("dve", lambda e, i=i: e.reciprocal(out=rstd[:, i:i + 1], in_=rstd[:, i:i + 1]), r=[f"rs{i}"], w=[f"rs{i}"])
        ph.add("dve", lambda e, i=i, o=o: e.scalar_tensor_tensor(out=o[:, :], in0=h[:, i, :], scalar=rstd[:, i:i + 1], in1=gb[:, :], op0=ALU.mult, op1=ALU.mult),
               r=[f"rs{i}", "gb"], w=[on])
        ph.add("sp", lambda e, i=i, o=o: e.dma_start(out=out_d[i * 128:(i + 1) * 128, :], in_=o[:, :]), r=[on], w=["out"], dma=True)
    ph.emit()
    return nc


def _bf(a):
    return np.asarray(a, dtype=np.float32).astype(ml_dtypes.bfloat16)


def _prep(inputs, nlayers=4):
    f32 = np.float32
    x = np.asarray(inputs["x"], f32)
    mem = np.asarray(inputs["mem"], f32)
    w_in_dil = np.asarray(inputs["w_in_dil"], f32)
    perm = np.concatenate([np.arange(h * 64 + 32, h * 64 + 64).tolist() + np.arange(h * 64, h * 64 + 32).tolist() for h in range(12)]).astype(np.int64)
    qs = w_in_dil[:, :, 0:768][:, :, perm]
    ks = w_in_dil[:, :, 768:1536][:, :, perm]
    w_in_dil_x = np.ascontiguousarray(np.concatenate([w_in_dil, qs, ks], axis=2))
    gains = np.zeros((13, 1024), f32)
    for l in range(4):
        gains[3 * l + 0] = inputs["norm_mix"][l]
        gains[3 * l + 1] = inputs["norm_mem"][l]
        gains[3 * l + 2] = inputs["norm_ffn"][l]
    gains_l = np.ascontiguousarray(gains.reshape(13, 8, 128).transpose(2, 0, 1).reshape(128, 104))
    cw = np.asarray(inputs["conv_w"], f32)
    convw = np.ascontiguousarray(cw.reshape(4, 3, 44, 128).transpose(0, 3, 1, 2).reshape(4, 128, 132))
    convb = np.ascontiguousarray(np.asarray(inputs["conv_b"], f32).reshape(4, 44, 128).transpose(0, 2, 1))
    p = np.arange(128)[:, None]
    f = np.arange(128)[None, :]
    ident = (p == f).astype(f32)
    tri = (p <= f).astype(f32)
    atri = (p >= f).astype(f32)
    cbf = _bf(np.concatenate([ident, tri, atri], axis=1))
    U = (p <= f).astype(f32)
    ones = np.ones((128, 128), f32)
    sel = np.zeros((128, 128), f32)
    sel[127, :] = 1.0
    cf32 = np.ascontiguousarray(np.concatenate([U, ones, sel], axis=1))
    inv = (1.0 / (np.float32(10000.0) ** (np.arange(0, 64, 2, dtype=f32) / np.float32(64)))).astype(f32)
    common = {
        "w_in_fox": np.ascontiguousarray(inputs["w_in_fox"], f32), "w_in_dil": w_in_dil_x,
        "w_mem_kv": np.ascontiguousarray(inputs["w_mem_kv"], f32), "w_out": np.ascontiguousarray(inputs["w_out"], f32),
        "w_up": np.ascontiguousarray(inputs["w_up"], f32), "w_down": np.ascontiguousarray(inputs["w_down"], f32),
        "gains": gains_l, "gfin": np.ascontiguousarray(np.asarray(inputs["norm_final"], f32).reshape(1, 1024)),
        "convw": convw, "convb": convb, "b_forget": np.ascontiguousarray(inputs["b_forget"], f32),
        "cbf": cbf, "cf32": cf32,
    }
    maps = []
    for c in range(8):
        b, hf = c // 2, c % 2
        pos = (np.arange(T, dtype=f32) + np.float32(hf * T))
        ang = pos[:, None] * inv[None, :]
        cos = np.cos(ang).astype(f32).T
        sin = np.sin(ang).astype(f32).T
        cosF = np.concatenate([cos, cos, cos, cos], axis=0)
        sinF = np.concatenate([-sin, sin, -sin, sin], axis=0)
        flags = np.zeros((128, 2), f32)
        flags[:, 0] = NEG if hf == 0 else 0.0
        flags[:, 1] = 0.0 if hf == 0 else 1.0
        m = dict(common)
        m["x"] = np.ascontiguousarray(x[b, hf * T:(hf + 1) * T])
        m["mem"] = np.ascontiguousarray(mem[b])
        m["rope"] = np.ascontiguousarray(np.stack([cosF, sinF], axis=0))
        m["flags"] = flags
        maps.append(m)
    return maps


_NC_CACHE = {}


def kernel(**inputs):
    if "nc" not in _NC_CACHE:
        _NC_CACHE["nc"] = build(4)
    nc = _NC_CACHE["nc"]
    maps = _prep(inputs)
    res = run_bass_kernel_spmd(nc, maps, core_ids=list(range(8)))
    out = np.zeros((4, 2 * T, D), np.float32)
    for c in range(8):
        out[c // 2, (c % 2) * T:(c % 2 + 1) * T] = res.results[c]["out"]
    return out
```

```python
import numpy as np
import ml_dtypes
import concourse.bass as bass
import concourse.mybir as mybir
from concourse.bass_utils import run_bass_kernel_spmd

F32 = mybir.dt.float32
BF16 = mybir.dt.bfloat16
AF = mybir.ActivationFunctionType
ALU = mybir.AluOpType

T = 2048
NT = 16
D = 1024
KC = 8
DFF = 2816
NJ = 22
XW = 2064
VW = 194
NEG = -30000.0
PAIRS = [[0, 1], [2, 3], [4, 5], [6, 7]]

SB_H = 16512
SB_C = SB_H + 65536
SB_X = SB_C + 9216
SB_Q = SB_X + 33024
SB_HD = SB_Q + 32768
SB_L = SB_HD + 32768
SB_END = 229344


def sl(start, count, step):
    return slice(start, start + (count - 1) * step + 1, step)


class Ctx:
    pass


class Phase:
    def __init__(self, cx):
        self.cx = cx
        self.ops = []

    def add(self, eng, fn, r=(), w=(), dma=False):
        self.ops.append([eng, fn, tuple(r), tuple(w), dma])

    def emit(self):
        cx = self.cx
        nc = cx.nc
        ops = self.ops
        lastw = {}
        readers = {}
        deps_all = []
        needed = set()
        for i, (eng, fn, r, w, dma) in enumerate(ops):
            raw = set()
            wxx = set()
            for b in r:
                if b in lastw:
                    raw.add(lastw[b])
            for b in w:
                if b in lastw:
                    wxx.add(lastw[b])
                wxx |= readers.get(b, set())
            raw.discard(i)
            wxx.discard(i)
            fd = set()
            for d in raw:
                de, _, _, _, ddma = ops[d]
                if (not ddma) and (not dma) and de == eng and eng == "pe":
                    continue
                fd.add(d)
            for d in wxx:
                de, _, _, _, ddma = ops[d]
                if (not ddma) and (not dma) and de == eng and eng == "pe":
                    continue
                fd.add(d)
            deps_all.append(fd)
            needed |= fd
            for b in r:
                readers.setdefault(b, set()).add(i)
            for b in w:
                lastw[b] = i
                readers[b] = set()
        sig = {}
        streams = {e: [] for e in ("pe", "act", "dve", "pool", "sp")}
        dma_used = {}
        for i, (eng, fn, r, w, dma) in enumerate(ops):
            wmax = {}
            for d in deps_all[i]:
                sm, vl = sig[d]
                if sm.name not in wmax or wmax[sm.name][1] < vl:
                    wmax[sm.name] = (sm, vl)
            waits = list(wmax.values())
            if dma:
                q = cx.dmaq[eng]
                slot = q["n"] % len(q["sems"])
                q["n"] += 1
                sem = q["sems"][slot]
                if q["cnt"][slot] > 0:
                    waits.append((sem, 16 * q["cnt"][slot]))
                q["cnt"][slot] += 1
                sig[i] = (sem, 16 * q["cnt"][slot])
                dma_used[(eng, slot)] = sig[i]
                streams[eng].append((waits, fn, sem, 16))
            else:
                if i in needed:
                    cx.cnt[eng] += 1
                    sig[i] = (cx.sem[eng], cx.cnt[eng])
                    streams[eng].append((waits, fn, cx.sem[eng], 1))
                else:
                    streams[eng].append((waits, fn, None, 0))
        finals = {e: [] for e in streams}
        for (eng, slot), s in dma_used.items():
            finals[eng].append(s)

        def mk(eng):
            lst = streams[eng]
            fin = finals[eng]
            waited = cx.waited[eng]

            def body(e):
                for waits, fn, sem, inc in lst:
                    for (s, v) in waits:
                        if waited.get(s.name, 0) < v:
                            e.wait_ge(s, v)
                            waited[s.name] = v
                    ins = fn(e)
                    if sem is not None:
                        ins.then_inc(sem, inc)
                for (s, v) in fin:
                    if waited.get(s.name, 0) < v:
                        e.wait_ge(s, v)
                        waited[s.name] = v
            return body

        with nc.Block() as blk:
            decos = {"pe": blk.tensor, "act": blk.scalar, "dve": blk.vector, "pool": blk.gpsimd, "sp": blk.sync}
            for eng in ("sp", "pool", "pe", "act", "dve"):
                if streams[eng] or finals[eng]:
                    decos[eng](mk(eng))
        self.ops = []


class Bump:
    def __init__(self, nc, regions, tag):
        self.nc = nc
        self.regions = [list(r) for r in regions]
        self.tag = tag
        self.k = 0

    def t(self, name, shape, dt):
        size = int(np.prod(shape[1:])) * (4 if dt == F32 else 2)
        size = (size + 31) // 32 * 32
        for rg in self.regions:
            if rg[0] + size <= rg[1]:
                off = rg[0]
                rg[0] += size
                self.k += 1
                return self.nc.alloc_sbuf_tensor_at(f"{self.tag}_{name}_{self.k}", list(shape), dt, offset=off)
        raise RuntimeError(f"SBUF bump overflow {self.tag} {name} {shape}")


def build(nlayers=4, dbg=False, stop=None):
    nc = bass.Bass("TRN2", target_bir_lowering=False)
    cx = Ctx()
    cx.nc = nc

    def din(name, shape, dt=F32):
        return nc.dram_tensor(name, list(shape), dt, kind="ExternalInput").ap()

    x_d = din("x", [T, D])
    mem_d = din("mem", [256, D])
    win_fox = din("w_in_fox", [2, D, 2572])
    win_dil = din("w_in_dil", [2, D, 4096])
    wmem_d = din("w_mem_kv", [4, D, 512])
    wout_d = din("w_out", [4, D, D])
    wup_d = din("w_up", [4, D, 2 * DFF])
    wdown_d = din("w_down", [4, DFF, D])
    gains_d = din("gains", [128, 13 * 8])
    gfin_d = din("gfin", [1, D])
    convw_d = din("convw", [4, 128, 3 * 44])
    convb_d = din("convb", [4, 128, 44])
    bfor_d = din("b_forget", [2, 12])
    cbf_d = din("cbf", [128, 384], BF16)
    cf32_d = din("cf32", [128, 384])
    rope_d = din("rope", [2, 128, T])
    flags_d = din("flags", [128, 2])
    out_d = nc.dram_tensor("out", [T, D], F32, kind="ExternalOutput").ap()
    dbg_d = None
    if dbg:
        dbg_d = nc.dram_tensor("dbg", [nlayers, T, D], F32, kind="ExternalOutput").ap()

    kmine = [nc.dram_tensor(f"kmine{g}", [384, T], BF16) for g in range(2)]
    kall = [nc.dram_tensor(f"kall{g}", [768, T], BF16) for g in range(2)]
    vmine = [nc.dram_tensor(f"vmine{g}", [T, 2 * VW], BF16) for g in range(3)]
    vall = [nc.dram_tensor(f"vall{g}", [2 * T, 2 * VW], BF16) for g in range(3)]
    emine = nc.dram_tensor("emine", [T, 12], F32)
    eall = nc.dram_tensor("eall", [2 * T, 12], F32)
    hmine = nc.dram_tensor("hmine", [128, 16], BF16)
    hall = nc.dram_tensor("hall", [256, 16], BF16)

    cx.sem = {}
    cx.cnt = {}
    cx.waited = {}
    cx.dmaq = {}
    for e in ("pe", "act", "dve", "pool", "sp"):
        cx.sem[e] = nc.alloc_semaphore(f"s_{e}")
        cx.cnt[e] = 0
        cx.waited[e] = {}
    for e, n in (("sp", 8), ("pool", 6), ("act", 4)):
        cx.dmaq[e] = {"sems": [nc.alloc_semaphore(f"d_{e}{i}") for i in range(n)], "cnt": [0] * n, "n": 0}
    ccsems = [nc.alloc_semaphore(f"cc{i}") for i in range(8 * nlayers)]

    h = nc.alloc_sbuf_tensor_at("h", [128, NT, D], F32, offset=SB_H)
    cb = Bump(nc, [(SB_C, SB_X)], "c")
    cbf = cb.t("cbf", [128, 384], BF16)
    ident = cbf[:, 0:128]
    tri = cbf[:, 128:256]
    dmask = cbf[:, 128:384]
    cf32 = cb.t("cf32", [128, 384], F32)
    Umat = cf32[:, 0:128]
    ones = cf32[:, 128:256]
    sel127 = cf32[:, 256:384]
    gains = cb.t("gains", [128, 13, 8], F32)
    convw = cb.t("convw", [128, 3, 44], F32)
    convb = cb.t("convb", [128, 44], F32)
    bfor = cb.t("bfor", [128, 12], F32)
    flags = cb.t("flags", [128, 2], F32)
    pmask = flags[:, 0:1]
    hflag = flags[:, 1:2]
    ss = cb.t("ss", [128, 32], F32)
    rstd = cb.t("rstd", [128, 32], F32)
    kmT = cb.t("kmT", [128, 2, 256], BF16)
    vaugm = cb.t("vaugm", [128, 2, 2, VW], BF16)
    dl = cb.t("dl", [128, NT, 12], F32)
    dmid = cb.t("dmid", [128, NT, 12], F32)
    epre = cb.t("epre", [128, NT, 12], F32)
    zcol = cb.t("zcol", [128, 2], F32)
    halo = cb.t("halo", [128, 8, 2], BF16)
    epsc = cb.t("epsc", [128, 2], F32)
    xnT = nc.alloc_sbuf_tensor_at("xnT", [128, KC, XW], BF16, offset=SB_X)
    qT = nc.alloc_sbuf_tensor_at("qT", [128, 8, T], BF16, offset=SB_Q)
    headsT = nc.alloc_sbuf_tensor_at("headsT", [128, 8, T], BF16, offset=SB_HD)
    mnT = nc.alloc_sbuf_tensor_at("mnT", [128, KC, 256], BF16, offset=SB_HD)

    pb = [nc.alloc_psum_tensor(f"pb{i}", [128, 512], F32) for i in range(8)]

    def pbank(i):
        return pb[i][:, :]

    def pcols(c0, w):
        assert c0 // 512 == (c0 + w - 1) // 512
        return pb[c0 // 512][:, c0 % 512: c0 % 512 + w]

    def pbank_bf(i):
        return pbank(i).bitcast(BF16)

    ph = Phase(cx)
    uid = [0]

    def U():
        uid[0] += 1
        return uid[0]

    ph.add("sp", lambda e: e.dma_start(out=cbf[:, :], in_=cbf_d[:, :]), w=["cbf"], dma=True)
    ph.add("sp", lambda e: e.dma_start(out=cf32[:, :], in_=cf32_d[:, :]), w=["cf32"], dma=True)
    ph.add("sp", lambda e: e.dma_start(out=gains[:, :, :], in_=gains_d.rearrange("p (a c) -> p a c", c=8)), w=["gains"], dma=True)
    ph.add("sp", lambda e: e.dma_start(out=flags[:, :], in_=flags_d[:, :]), w=["flags"], dma=True)
    for i4 in range(4):
        ph.add("sp", lambda e, i4=i4: e.dma_start(
            out=h[:, i4 * 4:(i4 + 1) * 4, :],
            in_=x_d[i4 * 512:(i4 + 1) * 512, :].rearrange("(n p) d -> p n d", p=128)), w=[f"h{i4}"], dma=True)
    ph.add("dve", lambda e: e.memset(zcol[:, :], 0.0), w=["zcol"])
    ph.add("dve", lambda e: e.memset(epsc[:, :], 1e-6), w=["epsc"])
    ph.add("dve", lambda e: e.memset(xnT[:, :, 0:2], 0.0), w=["xhalo"])
    if nlayers < 2:
        dum = cb.t("dum", [128, 4], F32)
        ph.add("sp", lambda e: e.dma_start(out=dum[:, 0:2], in_=rope_d[0][:, 0:2]), w=["dum0"], dma=True)
        ph.add("sp", lambda e: e.dma_start(out=dum[:, 2:4], in_=win_dil[0][0:128, 0:2]), w=["dum1"], dma=True)
    ph.emit()

    def rmsnorm_tile(ph, src, gidx, dst, loc, name, sidx):
        k = U()
        junk = loc["junk"]
        xh = loc["xh"][k % 2]
        xhn = f"xh{k % 2}"
        ph.add("act", lambda e: e.activation(out=junk[:, :], in_=src, func=AF.Square, accum_out=ss[:, sidx:sidx + 1]),
               r=[name], w=["junk", f"ss{sidx}"])
        ph.add("act", lambda e: e.activation(out=rstd[:, sidx:sidx + 1], in_=ss[:, sidx:sidx + 1], func=AF.Sqrt, scale=1.0 / D, bias=epsc[:, 0:1]),
               r=[f"ss{sidx}", "epsc"], w=[f"rs{sidx}"])
        ph.add("dve", lambda e: e.reciprocal(out=rstd[:, sidx:sidx + 1], in_=rstd[:, sidx:sidx + 1]), r=[f"rs{sidx}"], w=[f"rs{sidx}"])
        ph.add("dve", lambda e: e.tensor_scalar(out=xh[:, :], in0=src, scalar1=rstd[:, sidx:sidx + 1], scalar2=None,
                                                op0=ALU.mult), r=[name, f"rs{sidx}"], w=[xhn])
        bk = 4 + (k % 2)
        pt = pbank_bf(bk)
        for c in range(KC):
            ph.add("pe", lambda e, c=c: e.transpose(pt[:, c * 128:(c + 1) * 128], xh[:, c * 128:(c + 1) * 128], ident),
                   r=[xhn, "cbf"], w=[f"pbk{bk}"])
        g = gains[:, gidx, :].unsqueeze(2).to_broadcast([128, 8, 128])
        ph.add("dve", lambda e: e.tensor_tensor(out=dst, in0=pt[:, :].rearrange("p (c t) -> p c t", c=8), in1=g, op=ALU.mult),
               r=[f"pbk{bk}", "gains"], w=[f"nt_{name}"])

    for l in range(nlayers):
        fox = (l % 2 == 0)
        slot = l // 2
        win = win_fox[slot] if fox else win_dil[slot]

        lb = Bump(nc, [(SB_L, SB_END)], f"p1_{l}")
        loc = {"junk": lb.t("junk", [128, D], BF16), "xh": [lb.t("xh0", [128, D], BF16), lb.t("xh1", [128, D], BF16)]}
        memt = lb.t("memt", [128, 2, D], F32)
        ph.add("sp", lambda e: e.dma_start(out=memt[:, :, :], in_=mem_d.rearrange("(n p) d -> p n d", p=128)), w=["memt"], dma=True)
        ph.add("sp", lambda e, l=l: e.dma_start(out=convw[:, :, :], in_=convw_d[l].rearrange("p (a c) -> p a c", c=44)), w=["convw"], dma=True)
        ph.add("sp", lambda e, l=l: e.dma_start(out=convb[:, :], in_=convb_d[l]), w=["convb"], dma=True)
        if fox:
            ph.add("sp", lambda e, slot=slot: e.dma_start(out=bfor[:, :], in_=bfor_d[slot].partition_broadcast(128)), w=["bfor"], dma=True)
        for i in range(NT):
            rmsnorm_tile(ph, h[:, i, :], 3 * l + 0, xnT[:, :, 2 + i * 128: 2 + (i + 1) * 128], loc, f"h{i // 4}", i)
        for i in range(2):
            rmsnorm_tile(ph, memt[:, i, :], 3 * l + 1, mnT[:, :, i * 128:(i + 1) * 128], loc, "memt", 16 + i)
        ph.emit()
        if stop == "p1":
            break

        lb = Bump(nc, [(SB_HD + 4096, SB_L), (SB_L, SB_END)], f"p2_{l}")
        wq = [lb.t(f"wq{i}", [128, KC, 128], BF16) for i in range(4)]
        wv = [lb.t(f"wv{i}", [128, KC, 396], BF16) for i in range(2)]
        vst = [lb.t(f"vst{i}", [128, 6, VW], BF16) for i in range(2)]
        kst = [lb.t(f"kst{i}", [128, 512], BF16) for i in range(3)]
        wmv = lb.t("wmv", [128, KC, 256], BF16)
        if fox:
            flog = lb.t("flog", [128, NT, 12], F32)
            spb = lb.t("spb", [128, NT, 12], F32)
            totb = lb.t("totb", [128, NT, 12], F32)
            offb = lb.t("offb", [128, NT, 12], F32)
            exdb = lb.t("exdb", [128, NT, 12], F32)
        else:
            cosF = lb.t("cosF", [128, T], F32)
            sinF = lb.t("sinF", [128, T], F32)
            rt = [lb.t(f"rt{i}", [128, 512], F32) for i in range(4)]
            ph.add("sp", lambda e: e.dma_start(out=cosF[:, :], in_=rope_d[0]), w=["cosF"], dma=True)
            ph.add("sp", lambda e: e.dma_start(out=sinF[:, :], in_=rope_d[1]), w=["sinF"], dma=True)
        for i in range(2):
            ph.add("dve", lambda e, i=i: e.memset(vst[i][:, :, :], 0.0), w=[f"vst{i}"])
            ph.add("dve", lambda e, i=i: e.memset(vst[i][:, :, 64:65], 1.0), w=[f"vst{i}"])
            ph.add("dve", lambda e, i=i: e.memset(vst[i][:, :, 97:98], 1.0), w=[f"vst{i}"])

        wsrc = win.rearrange("(kc p) n -> p kc n", p=128)
        vcol0 = 1536
        vw = [384, 396 if fox else 384]
        for hf in range(2):
            ph.add("pool", lambda e, hf=hf: e.dma_start(out=wv[hf][:, :, 0:vw[hf]], in_=wsrc[:, :, vcol0 + hf * 384: vcol0 + hf * 384 + vw[hf]]),
                   w=[f"wv{hf}"], dma=True)

        qmcol = 2316 if fox else 2304
        chunks = []
        for c in range(6):
            chunks.append(("q", c, c * 128))
        for c in range(6):
            chunks.append(("k", c, 768 + c * 128))
        for c in range(2):
            chunks.append(("m", c, qmcol + c * 128))
        wk = [0]
        pk = [0]

        def load_w(col, src=None):
            i = wk[0] % 4
            wk[0] += 1
            s = wsrc if src is None else src
            ph.add("pool", lambda e: e.dma_start(out=wq[i][:, :, :], in_=s[:, :, col:col + 128]), w=[f"wq{i}"], dma=True)
            return i

        def proj_fm(wi, rhs_fn, n, bank):
            for kc in range(KC):
                ph.add("pe", lambda e, kc=kc: e.matmul(pbank(bank)[:, 0:n], wq[wi][:, kc, :], rhs_fn(kc), start=(kc == 0), stop=(kc == KC - 1)),
                       r=[f"wq{wi}", "xnT", "mnT"], w=[f"pbk{bank}"])

        kstk = [0]
        for (kind, c, col) in chunks:
            rope_on = (not fox) and kind in ("q", "k")
            wi = load_w(col)
            if rope_on:
                wis = load_w(2560 + (0 if kind == "q" else 768) + c * 128)
            for tc in range(4):
                bank = pk[0] % 2
                pk[0] += 1
                proj_fm(wi, lambda kc, tc=tc: xnT[:, kc, 2 + tc * 512: 2 + (tc + 1) * 512], 512, bank)
                if rope_on:
                    proj_fm(wis, lambda kc, tc=tc: xnT[:, kc, 2 + tc * 512: 2 + (tc + 1) * 512], 512, bank + 2)
                if kind == "k":
                    ks = kstk[0] % 3
                    kstk[0] += 1
                    dst = kst[ks][:, :]
                    dname = f"kst{ks}"
                else:
                    cc = c if kind == "q" else 6 + c
                    dst = qT[:, cc, tc * 512:(tc + 1) * 512]
                    dname = f"qT{cc}"
                if rope_on:
                    r0 = rt[(pk[0] % 2) * 2]
                    r1 = rt[(pk[0] % 2) * 2 + 1]
                    n0 = f"rt{(pk[0] % 2) * 2}"
                    n1 = f"rt{(pk[0] % 2) * 2 + 1}"
                    ph.add("dve", lambda e, bank=bank, tc=tc, r0=r0: e.tensor_tensor(out=r0[:, :], in0=pbank(bank), in1=cosF[:, tc * 512:(tc + 1) * 512], op=ALU.mult),
                           r=[f"pbk{bank}", "cosF"], w=[n0])
                    ph.add("dve", lambda e, bank=bank, tc=tc, r1=r1: e.tensor_tensor(out=r1[:, :], in0=pbank(bank + 2), in1=sinF[:, tc * 512:(tc + 1) * 512], op=ALU.mult),
                           r=[f"pbk{bank + 2}", "sinF"], w=[n1])
                    ph.add("dve", lambda e, r0=r0, r1=r1, dst=dst: e.tensor_tensor(out=dst, in0=r0[:, :], in1=r1[:, :], op=ALU.add),
                           r=[n0, n1], w=[dname])
                else:
                    ph.add("act", lambda e, bank=bank, dst=dst: e.activation(out=dst, in_=pbank(bank), func=AF.Copy), r=[f"pbk{bank}"], w=[dname])
                if kind == "k":
                    ph.add("sp", lambda e, c=c, tc=tc, dst=dst: e.dma_start(out=kmine[c // 3][(c % 3) * 128:(c % 3 + 1) * 128, tc * 512:(tc + 1) * 512], in_=dst),
                           r=[dname], w=["kmine"], dma=True)

        wmsrc = wmem_d[l].rearrange("(kc p) n -> p kc n", p=128)
        for c in range(2):
            wi = load_w(c * 128, src=wmsrc)
            bank = pk[0] % 2
            pk[0] += 1
            proj_fm(wi, lambda kc: mnT[:, kc, :], 256, bank)
            ph.add("act", lambda e, bank=bank, c=c: e.activation(out=kmT[:, c, :], in_=pbank(bank)[:, 0:256], func=AF.Copy), r=[f"pbk{bank}"], w=["kmT"])
        ph.add("pool", lambda e: e.dma_start(out=wmv[:, :, :], in_=wmsrc[:, :, 256:512]), w=["wmv"], dma=True)
        ph.add("dve", lambda e: e.memset(vaugm[:, :, :, :], 0.0), w=["vaugm"])
        ph.add("dve", lambda e: e.memset(vaugm[:, :, :, 64:65], 1.0), w=["vaugm"])
        ph.add("dve", lambda e: e.memset(vaugm[:, :, :, 97:98], 1.0), w=["vaugm"])
        for i in range(2):
            bank = pk[0] % 2
            pk[0] += 1
            for kc in range(KC):
                ph.add("pe", lambda e, kc=kc, i=i, bank=bank: e.matmul(pbank(bank)[:, 0:256], mnT[:, kc, i * 128:(i + 1) * 128], wmv[:, kc, :],
                                                                     start=(kc == 0), stop=(kc == KC - 1)), r=["mnT", "wmv"], w=[f"pbk{bank}"])
            pv = pbank(bank)[:, 0:256].rearrange("p (a b d) -> p a b d", a=2, b=2)
            ph.add("act", lambda e, i=i, pv=pv: e.activation(out=vaugm[:, i, :, 0:64], in_=pv[:, :, 0, :], func=AF.Copy), r=[f"pbk{bank}"], w=["vaugm"])
            ph.add("act", lambda e, i=i, pv=pv: e.activation(out=vaugm[:, i, :, 129:193], in_=pv[:, :, 1, :], func=AF.Copy), r=[f"pbk{bank}"], w=["vaugm"])

        for i in range(NT):
            vs = vst[i % 2]
            vsn = f"vst{i % 2}"
            for hf in range(2):
                bank = 2 + (pk[0] % 2)
                pk[0] += 1
                n = vw[hf]
                for kc in range(KC):
                    ph.add("pe", lambda e, kc=kc, i=i, hf=hf, bank=bank, n=n: e.matmul(pbank(bank)[:, 0:n], xnT[:, kc, 2 + i * 128: 2 + (i + 1) * 128], wv[hf][:, kc, 0:n],
                                                                                  start=(kc == 0), stop=(kc == KC - 1)), r=["xnT", f"wv{hf}"], w=[f"pbk{bank}"])
                pv = pbank(bank)[:, 0:384].rearrange("p (a b d) -> p a b d", a=3, b=2)
                ph.add("act", lambda e, vs=vs, hf=hf, pv=pv: e.activation(out=vs[:, hf * 3:(hf + 1) * 3, 0:64], in_=pv[:, :, 0, :], func=AF.Copy),
                       r=[f"pbk{bank}"], w=[vsn])
                ph.add("dve", lambda e, vs=vs, hf=hf, pv=pv: e.tensor_copy(out=vs[:, hf * 3:(hf + 1) * 3, 129:193], in_=pv[:, :, 1, :]),
                       r=[f"pbk{bank}"], w=[vsn])
                if fox and hf == 1:
                    ph.add("dve", lambda e, i=i, bank=bank: e.tensor_tensor(out=flog[:, i, :], in0=pbank(bank)[:, 384:396], in1=bfor[:, :], op=ALU.add),
                           r=[f"pbk{bank}", "bfor"], w=["flog"])
            for g in range(3):
                ph.add("sp", lambda e, i=i, vs=vs, g=g: e.dma_start(out=vmine[g][i * 128:(i + 1) * 128, :], in_=vs[:, 2 * g:2 * g + 2, :].rearrange("p a b -> p (a b)")),
                       r=[vsn], w=["vmine"], dma=True)

        if fox:
            fl2 = flog[:, :, :].rearrange("p a b -> p (a b)")
            sp2 = spb[:, :, :].rearrange("p a b -> p (a b)")
            ph.add("act", lambda e: e.activation(out=sp2, in_=fl2, func=AF.Exp, scale=-1.0), r=["flog"], w=["spb"])
            ph.add("act", lambda e: e.activation(out=sp2, in_=sp2, func=AF.Ln, bias=1.0), r=["spb"], w=["spb"])
            ph.add("pe", lambda e: e.matmul(pbank(0)[:, 0:192], Umat, sp2, start=True, stop=True), r=["spb", "cf32"], w=["pbk0"])
            ph.add("pe", lambda e: e.matmul(pbank(1)[:, 0:192], ones, sp2, start=True, stop=True), r=["spb", "cf32"], w=["pbk1"])
            ph.add("dve", lambda e: e.tensor_copy(out=totb[:, :, :].rearrange("p a b -> p (a b)"), in_=pbank(1)[:, 0:192]), r=["pbk1"], w=["totb"])
            ph.add("dve", lambda e: e.memset(offb[:, 0, :], 0.0), w=["offb"])
            for i in range(1, NT):
                ph.add("dve", lambda e, i=i: e.tensor_tensor(out=offb[:, i, :], in0=offb[:, i - 1, :], in1=totb[:, i - 1, :], op=ALU.add),
                       r=["offb", "totb"], w=["offb"])
            ph.add("dve", lambda e: e.tensor_tensor(out=dl[:, :, :].rearrange("p a b -> p (a b)"), in0=pbank(0)[:, 0:192],
                                                    in1=offb[:, :, :].rearrange("p a b -> p (a b)"), op=ALU.add), r=["pbk0", "offb"], w=["dl"])
            ph.add("pe", lambda e: e.matmul(pbank(2)[:, 0:192], sel127, dl[:, :, :].rearrange("p a b -> p (a b)"), start=True, stop=True),
                   r=["dl", "cf32"], w=["pbk2"])
            ph.add("dve", lambda e: e.tensor_copy(out=dmid[:, :, :].rearrange("p a b -> p (a b)"), in_=pbank(2)[:, 0:192]), r=["pbk2"], w=["dmid"])
            dtot = dmid[:, 15:16, :].to_broadcast([128, NT, 12])
            ph.add("dve", lambda e: e.tensor_tensor(out=exdb[:, :, :], in0=dl[:, :, :], in1=dtot, op=ALU.subtract), r=["dl", "dmid"], w=["exdb"])
            ph.add("sp", lambda e: e.dma_start(out=emine.ap().rearrange("(i p) h -> p i h", p=128), in_=exdb[:, :, :]), r=["exdb"], w=["emine"], dma=True)
        ph.emit()
        if stop == "p2":
            break

        cs = ccsems[8 * l: 8 * l + 8]

        def v2(t, a):
            return t.ap().rearrange("(a b) c -> a (b c)", a=a)

        def cc_body(e, cs=cs, fox=fox):
            for g in range(2):
                e.collective_compute("AllGather", ALU.bypass, replica_groups=PAIRS, ins=[v2(kmine[g], 128)], outs=[v2(kall[g], 256)]).then_inc(cs[g])
            for g in range(3):
                e.collective_compute("AllGather", ALU.bypass, replica_groups=PAIRS, ins=[v2(vmine[g], 128)], outs=[v2(vall[g], 256)]).then_inc(cs[2 + g])
            if fox:
                e.collective_compute("AllGather", ALU.bypass, replica_groups=PAIRS, ins=[v2(emine, 128)], outs=[v2(eall, 256)]).then_inc(cs[5])
            for g in range(6 if fox else 5):
                e.wait_ge(cs[g], 1)
        with nc.Block() as blk:
            blk.gpsimd(cc_body)
        if stop == "cc1":
            break

        lbx = Bump(nc, [(SB_X, SB_Q)], f"p3x_{l}")
        lbl = Bump(nc, [(SB_L, SB_END)], f"p3l_{l}")
        kTb = [lbx.t(f"kT{i}", [128, 2 * T], BF16) for i in range(2)]
        nfb = 2 if fox else 1
        bcs = [(lbx if fox else lbl).t(f"bcs{i}", [128, 512], F32) for i in range(nfb)]
        rden = [(lbx if fox else lbl).t(f"rden{i}", [128, 512], F32) for i in range(nfb)]
        fin_k = [0]

        class Pipe:
            def __init__(self):
                self.q = []

            def defer(self, n, fn):
                self.q.append([n, fn])

            def tick(self):
                for it in self.q:
                    it[0] -= 1
                ready = [it for it in self.q if it[0] <= 0]
                self.q = [it for it in self.q if it[0] > 0]
                for it in ready:
                    it[1]()

            def flush(self):
                while self.q:
                    self.tick()

        pipe = Pipe()
        SK = 3 if fox else 2
        SK_DIL = 2
        DIL_POOL_ONLY = True

        def finalize(acc_ap, accname, odd, dst, dname, sbuf_acc=False, delay=4):
            k = fin_k[0] % nfb
            fb = 6 + (fin_k[0] % 2) if fox else 7
            fin_k[0] += 1
            dp = 32 if odd else 64
            r0 = 64 if odd else 0
            if sbuf_acc:
                rd = acc_ap[dp:dp + 1, :]
                rdn = accname
            else:
                rd = rden[k][dp:dp + 1, :]
                rdn = f"rden{k}"
            ph.add("dve", lambda e: e.reciprocal(out=rd, in_=acc_ap[dp:dp + 1, :]), r=[accname], w=[rdn])

            def stage_b():
                ph.add("pe", lambda e: e.matmul(pbank(fb), ones[dp:dp + 1, :], rd, start=True, stop=True), r=[rdn, "cf32"], w=[f"pbk{fb}"])
                if fox:
                    ph.add("dve", lambda e: e.tensor_copy(out=bcs[k][r0:r0 + 64, :], in_=pbank(fb)[r0:r0 + 64, :]), r=[f"pbk{fb}"], w=[f"bcs{k}"])
                else:
                    ph.add("act", lambda e: e.activation(out=bcs[k][r0:r0 + 64, :], in_=pbank(fb)[r0:r0 + 64, :], func=AF.Copy), r=[f"pbk{fb}"], w=[f"bcs{k}"])
                ph.add("dve", lambda e: e.tensor_tensor(out=dst, in0=acc_ap[r0:r0 + 64, :], in1=bcs[k][r0:r0 + 64, :], op=ALU.mult),
                       r=[accname, f"bcs{k}"], w=[dname])
            stage_b()

        def vcols(odd):
            return (65, 193) if odd else (0, 65)

        acck = [0]
        sk = [0]
        ptk = [0]

        if fox:
            vgb = [lbl.t(f"vg{i}", [128, 32, VW], BF16) for i in range(2)]
            pT = [lbl.t(f"pT{i}", [128, 512], BF16) for i in range(6)]
            bt = [lbx.t(f"bt{i}", [128, 8, 32], F32) for i in range(4)]
            ph.add("sp", lambda e: e.dma_start(out=epre[:, :, :], in_=eall.ap()[0:T, :].rearrange("(i p) h -> p i h", p=128)), w=["epre"], dma=True)
            ph.add("dve", lambda e: e.tensor_scalar(out=epre[:, :, :], in0=epre[:, :, :], scalar1=pmask, scalar2=None, op0=ALU.add), r=["epre", "flags"], w=["epre"])
            btk = [0]
            for j in range(6):
                kb = kTb[j % 2]
                kn = f"kT{j % 2}"
                vg = vgb[j % 2]
                vn = f"vg{j % 2}"
                ph.add("sp", lambda e, j=j, kb=kb: e.dma_start(out=kb[:, 0:T], in_=kall[j // 3].ap()[(j % 3) * 128:(j % 3 + 1) * 128, :]), w=[kn], dma=True)
                ph.add("sp", lambda e, j=j, kb=kb: e.dma_start(out=kb[:, T:2 * T], in_=kmine[j // 3].ap()[(j % 3) * 128:(j % 3 + 1) * 128, :]), w=[kn], dma=True)
                for q4 in range(4):
                    ph.add("sp", lambda e, j=j, vg=vg, q4=q4: e.dma_start(out=vg[:, q4 * 4:(q4 + 1) * 4, :],
                           in_=vall[j // 2].ap()[q4 * 512:(q4 + 1) * 512, (j % 2) * VW:(j % 2 + 1) * VW].rearrange("(i p) w -> p i w", p=128)), w=[vn], dma=True)
                    ph.add("sp", lambda e, j=j, vg=vg, q4=q4: e.dma_start(out=vg[:, 16 + q4 * 4:16 + (q4 + 1) * 4, :],
                           in_=vmine[j // 2].ap()[q4 * 512:(q4 + 1) * 512, (j % 2) * VW:(j % 2 + 1) * VW].rearrange("(i p) w -> p i w", p=128)), w=[vn], dma=True)
                for hh in range(2):
                    hd = 2 * j + hh
                    odd = (hh == 1)
                    r0 = 64 * hh
                    bi = btk[0] % 4
                    btk[0] += 1
                    btt = bt[bi]
                    btn = f"bt{bi}"
                    for m in range(8):
                        ph.add("dve", lambda e, m=m, hd=hd, btt=btt: e.tensor_scalar(out=btt[:, m, 0:16], in0=epre[:, :, hd], scalar1=dmid[:, 2 * m, hd:hd + 1],
                                                                                   scalar2=None, op0=ALU.subtract), r=["epre", "dmid"], w=[btn + f"a{m}"])
                        ph.add("dve", lambda e, m=m, hd=hd, btt=btt: e.tensor_scalar(out=btt[:, m, 16:32], in0=dl[:, :, hd], scalar1=dmid[:, 2 * m, hd:hd + 1],
                                                                                   scalar2=None, op0=ALU.subtract), r=["dl", "dmid"], w=[btn + f"b{m}"])
                    v0, v1 = vcols(odd)
                    M = 128 if odd else 65
                    for qc in range(4):
                        ab = acck[0] % 2
                        acck[0] += 1
                        accn = f"pbk{ab}"
                        diag = [16 + 4 * qc + kj for kj in range(4)]
                        full = list(range(1, 16)) + list(range(16, 16 + 4 * qc))
                        order = [0] + diag + full
                        for idx, kt in enumerate(order):
                            kj = kt - (16 + 4 * qc) if kt in diag else 0
                            c0 = kj * 128
                            sb = 2 + (sk[0] % 4)
                            sk[0] += 1
                            ph.add("pe", lambda e, kb=kb, kt=kt, c0=c0, sb=sb, qc=qc, j=j, r0=r0: e.matmul(
                                pbank(sb)[:, c0:512], kb[r0:r0 + 64, kt * 128:(kt + 1) * 128], qT[r0:r0 + 64, j, qc * 512 + c0:(qc + 1) * 512],
                                start=True, stop=True), r=[kn, f"qT{j}"], w=[f"pbk{sb}"])
                            pi = ptk[0] % 6
                            ptk[0] += 1
                            pt_ = pT[pi]
                            ptn = f"pT{pi}"
                            rn = []
                            for hf in range(2):
                                a0 = max(c0, hf * 256)
                                a1 = (hf + 1) * 256
                                if a0 >= a1:
                                    continue
                                m = 2 * qc + hf
                                src_b = btn + (f"a{m}" if kt < 16 else f"b{m}")
                                ph.add("act", lambda e, sb=sb, a0=a0, a1=a1, m=m, kt=kt, pt_=pt_, btt=btt: e.activation(
                                    out=pt_[:, a0:a1], in_=pbank(sb)[:, a0:a1], func=AF.Exp, scale=0.125, bias=btt[:, m, kt:kt + 1]),
                                    r=[f"pbk{sb}", src_b], w=[ptn + f"h{hf}"])
                                rn.append(ptn + f"h{hf}")
                            if kt in diag:
                                hfd = c0 // 256
                                ph.add("pool", lambda e, pt_=pt_, c0=c0: e.tensor_tensor(out=pt_[:, c0:c0 + 128], in0=pt_[:, c0:c0 + 128], in1=tri, op=ALU.mult),
                                       r=[ptn + f"h{hfd}", "cbf"], w=[ptn + f"h{hfd}"])
                            last = (idx == len(order) - 1)

                            def pv(vg=vg, kt=kt, v0=v0, v1=v1, pt_=pt_, c0=c0, ab=ab, idx=idx, M=M, last=last, rn=tuple(rn), vn=vn, accn=accn,
                                   odd=odd, r0=r0, j=j, qc=qc):
                                ph.add("pe", lambda e: e.matmul(pbank(ab)[0:M, c0:512], vg[:, kt, v0:v1], pt_[:, c0:512], start=(idx == 0), stop=last),
                                       r=[vn] + list(rn), w=[accn])
                                if last:
                                    finalize(pbank(ab), accn, odd, headsT[r0:r0 + 64, j, qc * 512:(qc + 1) * 512], f"hT{j}")
                            pipe.defer(SK, pv)
                            pipe.tick()
        else:
            pipe.flush()
            accs = [lbx.t(f"accs{i}", [128, T], F32) for i in range(2)]
            pT = [lbl.t(f"pT{i}", [128, 256], BF16) for i in range(8)]
            BR = [(1, 17), (4, 20), (16, 32)]
            ntl = 17 + 20 + 32
            vgd = lbl.t("vgd", [128, ntl, VW], BF16)
            mk_eng = [0]
            for j in range(6):
                kb = kTb[j % 2]
                kn = f"kT{j % 2}"
                ph.add("sp", lambda e, j=j, kb=kb: e.dma_start(out=kb[:, 0:T], in_=kall[j // 3].ap()[(j % 3) * 128:(j % 3 + 1) * 128, :]), w=[kn], dma=True)
                ph.add("sp", lambda e, j=j, kb=kb: e.dma_start(out=kb[:, T:2 * T], in_=kmine[j // 3].ap()[(j % 3) * 128:(j % 3 + 1) * 128, :]), w=[kn], dma=True)
                tbase = {}
                ti = 0
                for (d, _) in BR:
                    nbo = 16 // d
                    for r in range(d):
                        tbase[(d, r)] = ti
                        p0 = (nbo - 1) * 128 * d + r
                        src = vall[j // 2].ap()[sl(p0, 128, d), (j % 2) * VW:(j % 2 + 1) * VW] if d > 1 else vall[j // 2].ap()[p0:p0 + 128, (j % 2) * VW:(j % 2 + 1) * VW]
                        ph.add("sp", lambda e, src=src, ti=ti: e.dma_start(out=vgd[:, ti, :], in_=src), w=[f"vgd{d}"], dma=True)
                        if d > 1:
                            src2 = vmine[j // 2].ap()[r: T: d, (j % 2) * VW:(j % 2 + 1) * VW].rearrange("(n p) w -> p n w", p=128)
                        else:
                            src2 = vmine[j // 2].ap()[:, (j % 2) * VW:(j % 2 + 1) * VW].rearrange("(n p) w -> p n w", p=128)
                        if nbo >= 8:
                            for s4 in range(nbo // 4):
                                ph.add("sp", lambda e, src2=src2, ti=ti, s4=s4: e.dma_start(out=vgd[:, ti + 1 + s4 * 4: ti + 1 + (s4 + 1) * 4, :], in_=src2[:, s4 * 4:(s4 + 1) * 4, :]),
                                       w=[f"vgd{d}"], dma=True)
                        else:
                            ph.add("sp", lambda e, src2=src2, ti=ti, nbo=nbo: e.dma_start(out=vgd[:, ti + 1: ti + 1 + nbo, :], in_=src2), w=[f"vgd{d}"], dma=True)
                        ti += 1 + nbo
                for hh in range(2):
                    odd = (hh == 1)
                    r0 = 64 * hh
                    v0, v1 = vcols(odd)
                    M = 128 if odd else 65
                    acs = accs[hh]
                    acn = f"accs{hh}"
                    for bi_, (d, _) in enumerate(BR):
                        nbo = 16 // d
                        units = [(r, kk) for r in range(d) for kk in range(nbo + 1)]
                        for ui, (r, kk) in enumerate(units):
                            tb = tbase[(d, r)]
                            if kk == 0:
                                kpos0 = (nbo - 1) * 128 * d + r
                            else:
                                kpos0 = T + (kk - 1) * 128 * d + r
                            qb0 = max(kk - 1, 0)
                            qb1 = min(kk, nbo - 1)
                            nq = (qb1 - qb0 + 1) * 128
                            qpos0 = qb0 * 128 * d + r
                            sb = 4 + (sk[0] % 3)
                            sk[0] += 1
                            ph.add("pe", lambda e, kb=kb, kpos0=kpos0, d=d, sb=sb, qpos0=qpos0, nq=nq, j=j, r0=r0: e.matmul(
                                pbank(sb)[:, 0:nq], kb[r0:r0 + 64, sl(kpos0, 128, d)], qT[r0:r0 + 64, j, sl(qpos0, nq, d)],
                                start=True, stop=True), r=[kn, f"qT{j}"], w=[f"pbk{sb}"])
                            pi = ptk[0] % 8
                            ptk[0] += 1
                            pt_ = pT[pi]
                            ptn = f"pT{pi}"
                            bias_ap = pmask if kk == 0 else zcol[:, 0:1]
                            ph.add("act", lambda e, sb=sb, nq=nq, pt_=pt_, bias_ap=bias_ap: e.activation(
                                out=pt_[:, 0:nq], in_=pbank(sb)[:, 0:nq], func=AF.Exp, scale=0.125, bias=bias_ap), r=[f"pbk{sb}", "flags", "zcol"], w=[ptn])
                            if kk == 0:
                                msk = dmask[:, 128:256]
                            elif kk == nbo:
                                msk = dmask[:, 0:128]
                            else:
                                msk = dmask[:, 0:256]
                            meng = "pool" if (DIL_POOL_ONLY or mk_eng[0] % 3 == 0) else "dve"
                            mk_eng[0] += 1
                            ph.add(meng, lambda e, pt_=pt_, nq=nq, msk=msk: e.tensor_tensor(out=pt_[:, 0:nq], in0=pt_[:, 0:nq], in1=msk, op=ALU.mult),
                                   r=[ptn, "cbf"], w=[ptn])
                            lastu = (ui == len(units) - 1)

                            def pv(kk=kk, tb=tb, v0=v0, v1=v1, pt_=pt_, ptn=ptn, qb0=qb0, qb1=qb1, r=r, nbo=nbo, M=M, lastu=lastu, d=d, bi_=bi_,
                                   acs=acs, acn=acn, odd=odd, r0=r0, j=j):
                                for qi, qb in enumerate(range(qb0, qb1 + 1)):
                                    same = (qb == kk - 1)
                                    col = (r * nbo + qb) * 128
                                    bnk = col // 512
                                    ph.add("pe", lambda e, qi=qi, col=col, same=same: e.matmul(
                                        pcols(col, 128)[0:M, :], vgd[:, tb + kk, v0:v1], pt_[:, qi * 128:(qi + 1) * 128], start=(not same), stop=same),
                                        r=[f"vgd{d}", ptn], w=[f"pbk{bnk}"])
                                if lastu:
                                    wdt = T // d
                                    for rr in range(d):
                                        for s in range(max(1, wdt // 512)):
                                            w_ = min(wdt, 512)
                                            c_src = rr * wdt + s * w_
                                            src = pcols(c_src, w_)
                                            if d == 1:
                                                dsta = acs[:, s * 512:(s + 1) * 512]
                                            else:
                                                dsta = acs[:, sl(rr + s * w_ * d, w_, d)]
                                            bnk = c_src // 512
                                            if bi_ == 0:
                                                ph.add("act", lambda e, src=src, dsta=dsta: e.activation(out=dsta, in_=src, func=AF.Copy), r=[f"pbk{bnk}"], w=[acn + f"s{s}"])
                                            else:
                                                ph.add("dve", lambda e, src=src, dsta=dsta: e.tensor_tensor(out=dsta, in0=src, in1=dsta, op=ALU.add),
                                                       r=[f"pbk{bnk}"] + [acn + f"s{x}" for x in range(4)], w=[acn + f"s{x}" for x in range(4)])
                                    if bi_ == 2:
                                        for qc in range(4):
                                            finalize(acs[:, qc * 512:(qc + 1) * 512], acn + f"s{qc}", odd, headsT[r0:r0 + 64, j, qc * 512:(qc + 1) * 512], f"hT{j}", sbuf_acc=True)
                            pipe.defer(SK_DIL, pv)
                            pipe.tick()
                pipe.flush()

        if True:
            pTm = [lbl.t(f"pTm{i}", [128, 512], BF16) for i in range(4)]
            mk = [0]
            nsb = 4 if fox else 3
            sb0 = 2 if fox else 4
            for j in range(2):
                for hh in range(2):
                    odd = (hh == 1)
                    r0 = 64 * hh
                    v0, v1 = vcols(odd)
                    M = 128 if odd else 65
                    for qc in range(4):
                        ab = acck[0] % 2
                        acck[0] += 1
                        for kt in range(2):
                            sb = sb0 + (sk[0] % nsb)
                            sk[0] += 1
                            pi = mk[0] % 4
                            mk[0] += 1
                            ph.add("pe", lambda e, j=j, kt=kt, sb=sb, qc=qc, r0=r0: e.matmul(
                                pbank(sb), kmT[r0:r0 + 64, j, kt * 128:(kt + 1) * 128], qT[r0:r0 + 64, 6 + j, qc * 512:(qc + 1) * 512], start=True, stop=True),
                                r=["kmT", f"qT{6 + j}"], w=[f"pbk{sb}"])
                            ph.add("act", lambda e, sb=sb, pi=pi: e.activation(out=pTm[pi][:, :], in_=pbank(sb), func=AF.Exp, scale=0.125, bias=zcol[:, 0:1]),
                                   r=[f"pbk{sb}", "zcol"], w=[f"pTm{pi}"])

                            def pvm(j=j, kt=kt, v0=v0, v1=v1, pi=pi, ab=ab, M=M, odd=odd, r0=r0, qc=qc):
                                ph.add("pe", lambda e: e.matmul(pbank(ab)[0:M, :], vaugm[:, kt, j, v0:v1], pTm[pi][:, :], start=(kt == 0), stop=(kt == 1)),
                                       r=["vaugm", f"pTm{pi}"], w=[f"pbk{ab}"])
                                if kt == 1:
                                    finalize(pbank(ab), f"pbk{ab}", odd, headsT[r0:r0 + 64, 6 + j, qc * 512:(qc + 1) * 512], f"hT{6 + j}", delay=2)
                            pipe.defer(SK, pvm)
                            pipe.tick()
        pipe.flush()
        ph.emit()
        if stop == "p3":
            break

        lb = Bump(nc, [(SB_Q, SB_HD), (SB_L, SB_END)], f"p4_{l}")
        wo = lb.t("wo", [128, KC, D], BF16)
        loc = {"junk": lb.t("junk", [128, D], BF16), "xh": [lb.t("xh0", [128, D], BF16), lb.t("xh1", [128, D], BF16)]}
        wosrc = wout_d[l].rearrange("(kc p) n -> p kc n", p=128)
        for kc in range(KC):
            ph.add("pool", lambda e, kc=kc: e.dma_start(out=wo[:, kc, :], in_=wosrc[:, kc, :]), w=[f"wo{kc}"], dma=True)
        ok = [0]

        def outproj_tile(i):
            for hf in range(2):
                bank = ok[0] % 2
                ok[0] += 1
                for kc in range(KC):
                    ph.add("pe", lambda e, kc=kc, i=i, hf=hf, bank=bank: e.matmul(pbank(bank), headsT[:, kc, i * 128:(i + 1) * 128], wo[:, kc, hf * 512:(hf + 1) * 512],
                                                                                  start=(kc == 0), stop=(kc == KC - 1)), r=[f"hT{kc}", f"wo{kc}"], w=[f"pbk{bank}"])
                ph.add("dve", lambda e, i=i, hf=hf, bank=bank: e.tensor_tensor(out=h[:, i, hf * 512:(hf + 1) * 512], in0=pbank(bank), in1=h[:, i, hf * 512:(hf + 1) * 512], op=ALU.add),
                       r=[f"pbk{bank}", f"ht{i}"], w=[f"ht{i}"])

        def norm2_tile(i):
            rmsnorm_tile(ph, h[:, i, :], 3 * l + 2, xnT[:, :, 2 + i * 128: 2 + (i + 1) * 128], loc, f"ht{i}", i)

        outproj_tile(NT - 1)
        norm2_tile(NT - 1)
        ph.add("sp", lambda e: e.dma_start(out=hmine.ap().rearrange("p (c t) -> p c t", t=2), in_=xnT[:, :, 2 + T - 2: 2 + T]), r=[f"nt_ht{NT - 1}"], w=["hmine"], dma=True)
        ph.emit()
        cs3 = ccsems[8 * l + 6]

        def cc2_body(e, cs3=cs3):
            e.collective_compute("AllGather", ALU.bypass, replica_groups=PAIRS, ins=[hmine.ap().opt()], outs=[hall.ap().opt()]).then_inc(cs3)
            e.wait_ge(cs3, 1)
        with nc.Block() as blk:
            blk.gpsimd(cc2_body)
        ph.add("sp", lambda e: e.dma_start(out=halo[:, :, :], in_=hall.ap()[0:128, :].rearrange("p (c t) -> p c t", t=2)), w=["halo"], dma=True)
        ph.add("dve", lambda e: e.tensor_scalar(out=xnT[:, :, 0:2], in0=halo[:, :, :], scalar1=hflag, scalar2=None, op0=ALU.mult), r=["halo", "flags"], w=["xhalo"])
        for i in range(NT - 1):
            outproj_tile(i)
            if i >= 1:
                norm2_tile(i - 1)
        norm2_tile(NT - 2)
        ph.emit()

        if stop == "p4":
            break
        wd = nc.alloc_sbuf_tensor_at(f"wd_{l}", [128, NJ, D], BF16, offset=SB_Q)
        gT = nc.alloc_sbuf_tensor_at(f"gT_{l}", [128, NJ, 1024], BF16, offset=SB_Q + 45056)
        lb = Bump(nc, [(SB_Q + 2 * 45056, SB_END)], f"p5_{l}")
        wu = [lb.t(f"wu{i}", [128, KC, 2, 128], BF16) for i in range(2)]
        cv = [lb.t(f"cv{i}", [128, 256], F32) for i in range(4)]
        sg = [lb.t(f"sg{i}", [128, 256], F32) for i in range(2)]
        wdsrc = wdown_d[l].rearrange("(j p) n -> p j n", p=128)
        pipe5 = Pipe()
        wusrc = wup_d[l].rearrange("(kc p) n -> p kc n", p=128)
        for tcb in range(2):
            for j in range(NJ):
                if tcb == 0 and j % 2 == 0:
                    ph.add("pool", lambda e, j=j: e.dma_start(out=wd[:, j:j + 2, :], in_=wdsrc[:, j:j + 2, :]), w=[f"wd{j}", f"wd{j + 1}"], dma=True)
                wi = (tcb * NJ + j) % 2
                ph.add("pool", lambda e, j=j, wi=wi: e.dma_start(out=wu[wi][:, :, 0, :], in_=wusrc[:, :, j * 128:(j + 1) * 128]), w=[f"wu{wi}"], dma=True)
                ph.add("pool", lambda e, j=j, wi=wi: e.dma_start(out=wu[wi][:, :, 1, :], in_=wusrc[:, :, DFF + j * 128: DFF + (j + 1) * 128]), w=[f"wu{wi}"], dma=True)
                for sub in range(4):
                    k = U()
                    tok0 = tcb * 1024 + sub * 256
                    pv_b = 2 + (k % 3) * 2
                    pg_b = pv_b + 1
                    for (vg_, bnk) in ((0, pv_b), (1, pg_b)):
                        for kc in range(KC):
                            ph.add("pe", lambda e, kc=kc, vg_=vg_, bnk=bnk, wi=wi, tok0=tok0: e.matmul(pbank(bnk)[:, 0:258], wu[wi][:, kc, vg_, :], xnT[:, kc, tok0:tok0 + 258],
                                                                                             start=(kc == 0), stop=(kc == KC - 1)), r=[f"wu{wi}", "xnT", "xhalo"], w=[f"pbk{bnk}"])
                    cvv = cv[(k % 2) * 2]
                    cvg = cv[(k % 2) * 2 + 1]
                    nv = f"cv{(k % 2) * 2}"
                    ng = f"cv{(k % 2) * 2 + 1}"
                    sgb = sg[k % 2]
                    sgn = f"sg{k % 2}"
                    vgl = ((0, pv_b, cvv, nv), (1, pg_b, cvg, ng))
                    for (vg_, bnk, cbuf, cn) in vgl:
                        jj = j + vg_ * NJ
                        ps = pbank(bnk)
                        ph.add("act", lambda e, ps=ps, cbuf=cbuf, jj=jj: e.activation(out=cbuf[:, :], in_=ps[:, 2:258], func=AF.Identity,
                                                                                      scale=convw[:, 2, jj:jj + 1], bias=convb[:, jj:jj + 1]),
                               r=[f"pbk{bnk}", "convw", "convb"], w=[cn])
                    for tap, c0_ in ((1, 1), (0, 0)):
                        for (vg_, bnk, cbuf, cn) in vgl:
                            jj = j + vg_ * NJ
                            ps = pbank(bnk)
                            ph.add("dve", lambda e, ps=ps, cbuf=cbuf, jj=jj, tap=tap, c0_=c0_: e.scalar_tensor_tensor(
                                out=cbuf[:, :], in0=ps[:, c0_:c0_ + 256], scalar=convw[:, tap, jj:jj + 1], in1=cbuf[:, :],
                                op0=ALU.mult, op1=ALU.add), r=[f"pbk{bnk}", "convw", cn], w=[cn])
                    def tail(cvg=cvg, sgb=sgb, cvv=cvv, j=j, sub=sub, ng=ng, sgn=sgn, nv=nv):
                        ph.add("act", lambda e: e.activation(out=sgb[:, :], in_=cvg[:, :], func=AF.Silu), r=[ng], w=[sgn])
                        ph.add("pool", lambda e: e.tensor_tensor(out=gT[:, j, sub * 256:(sub + 1) * 256], in0=sgb[:, :], in1=cvv[:, :], op=ALU.mult),
                               r=[sgn, nv], w=[f"gT{j}"])
                    pipe5.defer(2, tail)
                    pipe5.tick()
            pipe5.flush()
            for it in range(8):
                i = tcb * 8 + it
                for hf in range(2):
                    bank = (it * 2 + hf) % 2
                    for j in range(NJ):
                        ph.add("pe", lambda e, j=j, it=it, hf=hf, bank=bank: e.matmul(pbank(bank), gT[:, j, it * 128:(it + 1) * 128], wd[:, j, hf * 512:(hf + 1) * 512],
                                                                                      start=(j == 0), stop=(j == NJ - 1)), r=[f"gT{j}", f"wd{j}"], w=[f"pbk{bank}"])
                    ph.add("dve", lambda e, i=i, hf=hf, bank=bank: e.tensor_tensor(out=h[:, i, hf * 512:(hf + 1) * 512], in0=pbank(bank), in1=h[:, i, hf * 512:(hf + 1) * 512], op=ALU.add),
                           r=[f"pbk{bank}", f"hf{i}"], w=[f"hf{i}"])
        if dbg:
            for i4 in range(4):
                ph.add("sp", lambda e, i4=i4, l=l: e.dma_start(out=dbg_d[l, i4 * 512:(i4 + 1) * 512, :].rearrange("(n p) d -> p n d", p=128), in_=h[:, i4 * 4:(i4 + 1) * 4, :]),
                       r=[f"hf{i}" for i in range(i4 * 4, i4 * 4 + 4)], w=["dbg"], dma=True)
        ph.emit()

    lb = Bump(nc, [(SB_X, SB_END)], "fin")
    g1 = lb.t("g1", [128, D], F32)
    gb = lb.t("gb", [128, D], F32)
    junk = lb.t("junk", [128, D], F32)
    ob = [lb.t(f"ob{i}", [128, D], F32) for i in range(2)]
    ph.add("sp", lambda e: e.dma_start(out=g1[0:1, :], in_=gfin_d[:, :]), w=["g1"], dma=True)
    for hf in range(2):
        ph.add("pe", lambda e, hf=hf: e.matmul(pbank(hf), ones[0:1, :], g1[0:1, hf * 512:(hf + 1) * 512], start=True, stop=True), r=["g1", "cf32"], w=[f"pbk{hf}"])
        ph.add("act", lambda e, hf=hf: e.activation(out=gb[:, hf * 512:(hf + 1) * 512], in_=pbank(hf), func=AF.Copy), r=[f"pbk{hf}"], w=["gb"])
    for i in range(NT):
        o = ob[i % 2]
        on = f"ob{i % 2}"
        ph.add("act", lambda e, i=i: e.activation(out=junk[:, :], in_=h[:, i, :], func=AF.Square, accum_out=ss[:, i:i + 1]), w=["junk", f"ss{i}"])
        ph.add("act", lambda e, i=i: e.activation(out=rstd[:, i:i + 1], in_=ss[:, i:i + 1], func=AF.Sqrt, scale=1.0 / D, bias=epsc[:, 0:1]),
               r=[f"ss{i}"], w=[f"rs{i}"])
        ph.add("dve", lambda e, i=i: e.reciprocal(out=rstd[:, i:i + 1], in_=rstd[:, i:i + 1]), r=[f"rs{i}"], w=[f"rs{i}"])
        ph.add("dve", lambda e, i=i, o=o: e.scalar_tensor_tensor(out=o[:, :], in0=h[:, i, :], scalar=rstd[:, i:i + 1], in1=gb[:, :], op0=ALU.mult, op1=ALU.mult),
               r=[f"rs{i}", "gb"], w=[on])
        ph.add("sp", lambda e, i=i, o=o: e.dma_start(out=out_d[i * 128:(i + 1) * 128, :], in_=o[:, :]), r=[on], w=["out"], dma=True)
    ph.emit()
    return nc


def _bf(a):
    return np.asarray(a, dtype=np.float32).astype(ml_dtypes.bfloat16)


def _prep(inputs, nlayers=4):
    f32 = np.float32
    x = np.asarray(inputs["x"], f32)
    mem = np.asarray(inputs["mem"], f32)
    w_in_dil = np.asarray(inputs["w_in_dil"], f32)
    perm = np.concatenate([np.arange(h * 64 + 32, h * 64 + 64).tolist() + np.arange(h * 64, h * 64 + 32).tolist() for h in range(12)]).astype(np.int64)
    qs = w_in_dil[:, :, 0:768][:, :, perm]
    ks = w_in_dil[:, :, 768:1536][:, :, perm]
    w_in_dil_x = np.ascontiguousarray(np.concatenate([w_in_dil, qs, ks], axis=2))
    gains = np.zeros((13, 1024), f32)
    for l in range(4):
        gains[3 * l + 0] = inputs["norm_mix"][l]
        gains[3 * l + 1] = inputs["norm_mem"][l]
        gains[3 * l + 2] = inputs["norm_ffn"][l]
    gains_l = np.ascontiguousarray(gains.reshape(13, 8, 128).transpose(2, 0, 1).reshape(128, 104))
    cw = np.asarray(inputs["conv_w"], f32)
    convw = np.ascontiguousarray(cw.reshape(4, 3, 44, 128).transpose(0, 3, 1, 2).reshape(4, 128, 132))
    convb = np.ascontiguousarray(np.asarray(inputs["conv_b"], f32).reshape(4, 44, 128).transpose(0, 2, 1))
    p = np.arange(128)[:, None]
    f = np.arange(128)[None, :]
    ident = (p == f).astype(f32)
    tri = (p <= f).astype(f32)
    atri = (p >= f).astype(f32)
    cbf = _bf(np.concatenate([ident, tri, atri], axis=1))
    U = (p <= f).astype(f32)
    ones = np.ones((128, 128), f32)
    sel = np.zeros((128, 128), f32)
    sel[127, :] = 1.0
    cf32 = np.ascontiguousarray(np.concatenate([U, ones, sel], axis=1))
    inv = (1.0 / (np.float32(10000.0) ** (np.arange(0, 64, 2, dtype=f32) / np.float32(64)))).astype(f32)
    common = {
        "w_in_fox": np.ascontiguousarray(inputs["w_in_fox"], f32), "w_in_dil": w_in_dil_x,
        "w_mem_kv": np.ascontiguousarray(inputs["w_mem_kv"], f32), "w_out": np.ascontiguousarray(inputs["w_out"], f32),
        "w_up": np.ascontiguousarray(inputs["w_up"], f32), "w_down": np.ascontiguousarray(inputs["w_down"], f32),
        "gains": gains_l, "gfin": np.ascontiguousarray(np.asarray(inputs["norm_final"], f32).reshape(1, 1024)),
        "convw": convw, "convb": convb, "b_forget": np.ascontiguousarray(inputs["b_forget"], f32),
        "cbf": cbf, "cf32": cf32,
    }
    maps = []
    for c in range(8):
        b, hf = c // 2, c % 2
        pos = (np.arange(T, dtype=f32) + np.float32(hf * T))
        ang = pos[:, None] * inv[None, :]
        cos = np.cos(ang).astype(f32).T
        sin = np.sin(ang).astype(f32).T
        cosF = np.concatenate([cos, cos, cos, cos], axis=0)
        sinF = np.concatenate([-sin, sin, -sin, sin], axis=0)
        flags = np.zeros((128, 2), f32)
        flags[:, 0] = NEG if hf == 0 else 0.0
        flags[:, 1] = 0.0 if hf == 0 else 1.0
        m = dict(common)
        m["x"] = np.ascontiguousarray(x[b, hf * T:(hf + 1) * T])
        m["mem"] = np.ascontiguousarray(mem[b])
        m["rope"] = np.ascontiguousarray(np.stack([cosF, sinF], axis=0))
        m["flags"] = flags
        maps.append(m)
    return maps


_NC_CACHE = {}


def kernel(**inputs):
    if "nc" not in _NC_CACHE:
        _NC_CACHE["nc"] = build(4)
    nc = _NC_CACHE["nc"]
    maps = _prep(inputs)
    res = run_bass_kernel_spmd(nc, maps, core_ids=list(range(8)))
    out = np.zeros((4, 2 * T, D), np.float32)
    for c in range(8):
        out[c // 2, (c % 2) * T:(c % 2 + 1) * T] = res.results[c]["out"]
    return out
```
